# Optimizing a Trainium2 kernel written in Bass

```python
import math
import jax
import jax.numpy as jnp
from jax import lax
import numpy as np


D_MODEL = 1024
BATCH = 32
SEQ = 2048
DEPTH = 2

CTX_LEN = 256
GRID_W = 64
HEAD_DIM = 64
ROPE_BASE = 10000.0
EPS = 1e-6
NEG_INF = -1e30
ATTN_SCALE = HEAD_DIM ** -0.5
Q_BLOCK = 128
N_EVEN = (DEPTH + 1) // 2
N_ODD = DEPTH // 2

A_HEADS = 8
A_KV_HEADS = 2
A_GROUP = A_HEADS // A_KV_HEADS
WINDOW = 128
A_WIDTH = A_HEADS * HEAD_DIM
B_HEADS = 4
B_VDIM = 2 * HEAD_DIM
B_WIDTH = B_HEADS * B_VDIM
EVEN_SPLITS = (A_WIDTH, A_KV_HEADS * HEAD_DIM, A_KV_HEADS * HEAD_DIM, A_WIDTH,
               B_HEADS * 2 * HEAD_DIM, B_HEADS * 2 * HEAD_DIM, B_WIDTH, B_WIDTH)
EVEN_IN = sum(EVEN_SPLITS)

G_HEADS = 4
G_DK = (D_MODEL // 2) // G_HEADS
G_DV = D_MODEL // G_HEADS
G_RANK = 16
G_TAU = 16.0
G_CHUNK = 64
ODD_SPLITS = (G_HEADS * G_DK, G_HEADS * G_DK, G_HEADS * G_DV, G_HEADS * G_DV, G_RANK, G_RANK)
ODD_IN = sum(ODD_SPLITS)

kernel_name = 'hybrid_window_diff_gla_prefix_dit'


def rms_norm(x, g):
    xf = x.astype(jnp.float32)
    y = xf * lax.rsqrt(jnp.mean(xf * xf, axis=-1, keepdims=True) + EPS)
    return (y * g.astype(jnp.float32)).astype(x.dtype)


def split_cols(p, sizes):
    out, start = [], 0
    for s in sizes:
        out.append(p[..., start:start + s])
        start += s
    return out


def adaln_params(cond, w, b):
    m = jax.nn.silu(cond) @ w + b
    return jnp.split(m, 3, axis=-1)


def grid_positions(n_rows):
    rows = jnp.repeat(jnp.arange(n_rows, dtype=jnp.int32), GRID_W)
    cols = jnp.tile(jnp.arange(GRID_W, dtype=jnp.int32), n_rows)
    return rows, cols


def rope_1d(x, pos):
    m = x.shape[-1] // 2
    inv = ROPE_BASE ** (-jnp.arange(m, dtype=jnp.float32) / m)
    ang = pos.astype(jnp.float32)[:, None] * inv[None, :]
    cos = jnp.cos(ang)[:, None, :]
    sin = jnp.sin(ang)[:, None, :]
    x1 = x[..., :m].astype(jnp.float32)
    x2 = x[..., m:].astype(jnp.float32)
    return jnp.concatenate([x1 * cos - x2 * sin, x2 * cos + x1 * sin], axis=-1)


def rope_axial(x, rows, cols):
    h = x.shape[-1] // 2
    return jnp.concatenate([rope_1d(x[..., :h], rows), rope_1d(x[..., h:], cols)], axis=-1).astype(x.dtype)


def sink_attention(qg, k, v, sink, valid):
    s = jnp.einsum('bqhgd,bkhd->bhgqk', qg, k).astype(jnp.float32) * ATTN_SCALE
    if valid is not None:
        s = jnp.where(valid, s, NEG_INF)
    sink_col = jnp.broadcast_to(sink.astype(jnp.float32).reshape(1, A_KV_HEADS, A_GROUP, 1, 1),
                                s.shape[:-1] + (1,))
    p = jax.nn.softmax(jnp.concatenate([s, sink_col], axis=-1), axis=-1)[..., :-1]
    return jnp.einsum('bhgqk,bkhd->bqhgd', p, v.astype(jnp.float32))


def window_sink_attention(q, k, v, kc, vc, sink):
    Bn, T = q.shape[:2]
    nb = T // Q_BLOCK
    band = Q_BLOCK + 2 * WINDOW
    qg = q.reshape(Bn, T, A_KV_HEADS, A_GROUP, HEAD_DIM)
    pad = ((0, 0), (WINDOW, WINDOW), (0, 0), (0, 0))
    kp = jnp.pad(k, pad)
    vp = jnp.pad(v, pad)
    ctx_valid = jnp.ones((Q_BLOCK, kc.shape[1]), dtype=bool)

    def block(i):
        start = i * Q_BLOCK
        qb = lax.dynamic_slice_in_dim(qg, start, Q_BLOCK, axis=1)
        kb = lax.dynamic_slice_in_dim(kp, start, band, axis=1)
        vb = lax.dynamic_slice_in_dim(vp, start, band, axis=1)
        qpos = start + jnp.arange(Q_BLOCK)
        kpos = start - WINDOW + jnp.arange(band)
        valid = ((kpos[None, :] >= 0) & (kpos[None, :] < T)
                 & (jnp.abs(qpos[:, None] - kpos[None, :]) <= WINDOW))
        valid = jnp.concatenate([valid, ctx_valid], axis=1)
        return sink_attention(qb, jnp.concatenate([kb, kc], axis=1),
                              jnp.concatenate([vb, vc], axis=1), sink, valid)

    out = lax.map(block, jnp.arange(nb))
    return jnp.moveaxis(out, 0, 1).reshape(Bn, T, A_KV_HEADS, A_GROUP, HEAD_DIM)


def diff_attention(q1, q2, k1, k2, v, lam):
    s1 = jnp.einsum('bqhd,bkhd->bhqk', q1, k1).astype(jnp.float32) * ATTN_SCALE
    s2 = jnp.einsum('bqhd,bkhd->bhqk', q2, k2).astype(jnp.float32) * ATTN_SCALE
    w = jax.nn.softmax(s1, axis=-1) - lam * jax.nn.softmax(s2, axis=-1)
    return jnp.einsum('bhqk,bkhe->bqhe', w, v.astype(jnp.float32))


def blocked_diff_attention(q1, q2, k1, k2, v, lam):
    Bn, T = q1.shape[:2]
    nb = T // Q_BLOCK

    def block(i):
        start = i * Q_BLOCK
        return diff_attention(lax.dynamic_slice_in_dim(q1, start, Q_BLOCK, axis=1),
                              lax.dynamic_slice_in_dim(q2, start, Q_BLOCK, axis=1),
                              k1, k2, v, lam)

    out = lax.map(block, jnp.arange(nb))
    return jnp.moveaxis(out, 0, 1).reshape(Bn, T, B_HEADS, B_VDIM)


def even_mixer(hx, hc, rows, cols, w_in, a_qn, a_kn, a_sink, b_qn, b_kn,
               lq1, lk1, lq2, lk2, b_subln, lambda_init, need_ctx):
    dt = hx.dtype

    def project(h):
        Bn, T = h.shape[:2]
        qa, ka, va, ga, qb, kb, vb, gb = split_cols(h @ w_in, EVEN_SPLITS)
        qa = rms_norm(qa.reshape(Bn, T, A_HEADS, HEAD_DIM), a_qn)
        ka = rms_norm(ka.reshape(Bn, T, A_KV_HEADS, HEAD_DIM), a_kn)
        va = va.reshape(Bn, T, A_KV_HEADS, HEAD_DIM)
        qb = rms_norm(qb.reshape(Bn, T, 2 * B_HEADS, HEAD_DIM), b_qn)
        kb = rms_norm(kb.reshape(Bn, T, 2 * B_HEADS, HEAD_DIM), b_kn)
        vb = vb.reshape(Bn, T, B_HEADS, B_VDIM)
        return qa, ka, va, ga, qb, kb, vb, gb

    def split_pair(t):
        Bn, T = t.shape[:2]
        t = t.reshape(Bn, T, B_HEADS, 2, HEAD_DIM)
        return t[..., 0, :], t[..., 1, :]

    def merge(oa, ga, ob, gb):
        Bn, T = oa.shape[:2]
        ob = rms_norm(ob, b_subln) * (1.0 - lambda_init)
        oa = oa.reshape(Bn, T, A_WIDTH).astype(dt) * jax.nn.silu(ga)
        ob = ob.reshape(Bn, T, B_WIDTH).astype(dt) * jax.nn.silu(gb)
        return jnp.concatenate([oa, ob], axis=-1)

    qa_x, ka_x, va_x, ga_x, qb_x, kb_x, vb_x, gb_x = project(hx)
    qa_c, ka_c, va_c, ga_c, qb_c, kb_c, vb_c, gb_c = project(hc)
    qa_x = rope_axial(qa_x, rows, cols)
    ka_x = rope_axial(ka_x, rows, cols)
    qb_x = rope_axial(qb_x, rows, cols)
    kb_x = rope_axial(kb_x, rows, cols)

    lam = (jnp.exp(jnp.sum(lq1.astype(jnp.float32) * lk1.astype(jnp.float32)))
           - jnp.exp(jnp.sum(lq2.astype(jnp.float32) * lk2.astype(jnp.float32))) + lambda_init)

    q1x, q2x = split_pair(qb_x)
    k1x, k2x = split_pair(kb_x)
    k1c, k2c = split_pair(kb_c)

    oa_x = window_sink_attention(qa_x, ka_x, va_x, ka_c, va_c, a_sink)
    ob_x = blocked_diff_attention(q1x, q2x,
                                  jnp.concatenate([k1x, k1c], axis=1),
                                  jnp.concatenate([k2x, k2c], axis=1),
                                  jnp.concatenate([vb_x, vb_c], axis=1), lam)
    ox = merge(oa_x, ga_x, ob_x, gb_x)
    if not need_ctx:
        return ox, None
    Bn, L = hc.shape[:2]
    oa_c = sink_attention(qa_c.reshape(Bn, L, A_KV_HEADS, A_GROUP, HEAD_DIM), ka_c, va_c, a_sink, None)
    q1c, q2c = split_pair(qb_c)
    ob_c = diff_attention(q1c, q2c, k1c, k2c, vb_c, lam)
    oc = merge(oa_c, ga_c, ob_c, gb_c)
    return ox, oc


def gla_chunked(q, k, v, log_a, s0, with_output):
    Bn, H, T, dk = q.shape
    dv = v.shape[-1]
    n = T // G_CHUNK
    f32 = jnp.float32

    def ch(t):
        return t.astype(f32).reshape(Bn, H, n, G_CHUNK, t.shape[-1])

    q, k, v, la = ch(q), ch(k), ch(v), ch(log_a)
    b = jnp.cumsum(la, axis=3)
    b_last = b[:, :, :, -1:, :]
    d_state = jnp.einsum('bhncd,bhnce->bhnde', k * jnp.exp(b_last - b), v)
    decay = jnp.exp(b_last[:, :, :, 0, :])

    def step(S, inp):
        dec, ds = inp
        return dec[..., None] * S + ds, S

    s_fin, s_prev = lax.scan(step, s0.astype(f32),
                             (jnp.moveaxis(decay, 2, 0), jnp.moveaxis(d_state, 2, 0)))
    if not with_output:
        return None, s_fin
    s_prev = jnp.moveaxis(s_prev, 0, 2)
    qd = q * jnp.exp(b)
    kd = k * jnp.exp(-b)
    tri = jnp.tril(jnp.ones((G_CHUNK, G_CHUNK), dtype=bool))
    att = jnp.where(tri, jnp.einsum('bhncd,bhnsd->bhncs', qd, kd), 0.0)
    o = (jnp.einsum('bhncs,bhnse->bhnce', att, v)
         + jnp.einsum('bhncd,bhnde->bhnce', qd, s_prev))
    return o.reshape(Bn, H, T, dv), s_fin


def gla_mixer(hx, hc, w_in, wa_f, ba_f, wa_b, ba_b, out_norm, need_ctx):
    dt = hx.dtype

    def project(h):
        Bn, T = h.shape[:2]
        q, k, v, g, rf, rb = split_cols(h @ w_in, ODD_SPLITS)

        def heads(t, d):
            return t.reshape(Bn, T, G_HEADS, d).transpose(0, 2, 1, 3)

        la_f = jax.nn.log_sigmoid((rf @ wa_f + ba_f).astype(jnp.float32)) / G_TAU
        la_b = jax.nn.log_sigmoid((rb @ wa_b + ba_b).astype(jnp.float32)) / G_TAU
        return (heads(q, G_DK) * (G_DK ** -0.5), heads(k, G_DK), heads(v, G_DV), g,
                heads(la_f, G_DK), heads(la_b, G_DK))

    def flip(t):
        return jnp.flip(t, axis=2)

    def finish(o, g):
        Bn, H, T, dv = o.shape
        o = rms_norm(o.transpose(0, 2, 1, 3), out_norm).reshape(Bn, T, H * dv)
        return o.astype(dt) * jax.nn.silu(g)

    qx, kx, vx, gx, lfx, lbx = project(hx)
    qc, kc, vc, gc, lfc, lbc = project(hc)
    Bn = hx.shape[0]
    s0 = jnp.zeros((Bn, G_HEADS, G_DK, G_DV), jnp.float32)
    oc_f, sc_f = gla_chunked(qc, kc, vc, lfc, s0, need_ctx)
    oc_b, sc_b = gla_chunked(flip(qc), flip(kc), flip(vc), flip(lbc), s0, need_ctx)
    ox_f, _ = gla_chunked(qx, kx, vx, lfx, sc_f, True)
    ox_b, _ = gla_chunked(flip(qx), flip(kx), flip(vx), flip(lbx), sc_b, True)
    ox = finish(ox_f + flip(ox_b), gx)
    if not need_ctx:
        return ox, None
    oc = finish(oc_f + flip(oc_b), gc)
    return ox, oc


def setup_inputs(seed: int = 0) -> dict:
    key = jax.random.key(seed)
    ks = jax.random.split(key, 32)
    f32 = jnp.float32

    def nrm(k, shape, scale):
        return jax.random.normal(k, shape, f32) * scale

    def gain(k, shape):
        return 1.0 + 0.1 * jax.random.normal(k, shape, f32)

    D = D_MODEL
    return {
        'x': nrm(ks[0], (BATCH, SEQ, D), 1.0),
        'c': nrm(ks[1], (BATCH, D), 1.0),
        'ctx': nrm(ks[2], (BATCH, CTX_LEN, D), 1.0),
        'c_ctx': nrm(ks[3], (D,), 1.0),
        'adaln_w': nrm(ks[4], (DEPTH, D, 3 * D), 0.5 * D ** -0.5),
        'adaln_b': nrm(ks[5], (DEPTH, 3 * D), 0.02),
        'norm_g': gain(ks[6], (DEPTH, D)),
        'w_out': nrm(ks[7], (DEPTH, D, D), D ** -0.5),
        'ab_w_in': nrm(ks[8], (N_EVEN, D, EVEN_IN), D ** -0.5),
        'a_q_norm': gain(ks[9], (N_EVEN, HEAD_DIM)),
        'a_k_norm': gain(ks[10], (N_EVEN, HEAD_DIM)),
        'a_sink': nrm(ks[11], (N_EVEN, A_HEADS), 0.5),
        'b_q_norm': gain(ks[12], (N_EVEN, HEAD_DIM)),
        'b_k_norm': gain(ks[13], (N_EVEN, HEAD_DIM)),
        'b_lambda_q1': nrm(ks[14], (N_EVEN, HEAD_DIM), 0.1),
        'b_lambda_k1': nrm(ks[15], (N_EVEN, HEAD_DIM), 0.1),
        'b_lambda_q2': nrm(ks[16], (N_EVEN, HEAD_DIM), 0.1),
        'b_lambda_k2': nrm(ks[17], (N_EVEN, HEAD_DIM), 0.1),
        'b_subln': gain(ks[18], (N_EVEN, B_VDIM)),
        'gla_w_in': nrm(ks[19], (N_ODD, D, ODD_IN), D ** -0.5),
        'gla_wa_f': nrm(ks[20], (N_ODD, G_RANK, G_HEADS * G_DK), G_RANK ** -0.5),
        'gla_ba_f': nrm(ks[21], (N_ODD, G_HEADS * G_DK), 0.1),
        'gla_wa_b': nrm(ks[22], (N_ODD, G_RANK, G_HEADS * G_DK), G_RANK ** -0.5),
        'gla_ba_b': nrm(ks[23], (N_ODD, G_HEADS * G_DK), 0.1),
        'gla_out_norm': gain(ks[24], (N_ODD, G_DV)),
    }


def reference(x, c, ctx, c_ctx, adaln_w, adaln_b, norm_g, w_out, ab_w_in, a_q_norm, a_k_norm,
              a_sink, b_q_norm, b_k_norm, b_lambda_q1, b_lambda_k1, b_lambda_q2, b_lambda_k2,
              b_subln, gla_w_in, gla_wa_f, gla_ba_f, gla_wa_b, gla_ba_b, gla_out_norm):
    ROWS = x.shape[1] // GRID_W
    rows, cols = grid_positions(ROWS)
    for layer in range(DEPTH):
        need_ctx = layer < DEPTH - 1
        shx, scx, gtx = adaln_params(c, adaln_w[layer], adaln_b[layer])
        shc, scc, gtc = adaln_params(c_ctx, adaln_w[layer], adaln_b[layer])
        hx = rms_norm(x, norm_g[layer]) * (1.0 + scx[:, None, :]) + shx[:, None, :]
        hc = rms_norm(ctx, norm_g[layer]) * (1.0 + scc) + shc
        j = layer // 2
        if layer % 2 == 0:
            lambda_init = 0.8 - 0.6 * math.exp(-0.3 * layer)
            ox, oc = even_mixer(hx, hc, rows, cols, ab_w_in[j], a_q_norm[j], a_k_norm[j], a_sink[j],
                                b_q_norm[j], b_k_norm[j], b_lambda_q1[j], b_lambda_k1[j],
                                b_lambda_q2[j], b_lambda_k2[j], b_subln[j], lambda_init, need_ctx)
        else:
            ox, oc = gla_mixer(hx, hc, gla_w_in[j], gla_wa_f[j], gla_ba_f[j], gla_wa_b[j], gla_ba_b[j],
                               gla_out_norm[j], need_ctx)
        x = x + gtx[:, None, :] * (ox @ w_out[layer])
        if need_ctx:
            ctx = ctx + gtc * (oc @ w_out[layer])
    return x
```

```python
import math
from contextlib import ExitStack

import numpy as np
import concourse.bass as bass
import concourse.mybir as mybir
from concourse.bass_utils import run_bass_kernel_spmd
from concourse.alu_op_type import AluOpType as ALU

F32 = mybir.dt.float32
BF16 = mybir.dt.bfloat16
U8 = mybir.dt.uint8
AF = mybir.ActivationFunctionType
AXX = mybir.AxisListType.X

D = 1024
T = 2048
LC = 256
NTX = 16
NT = 18
EPS = 1e-6
E_IN = 3328
O_IN = 3104
ARENA_BYTES = 200 * 1024


def _dsize(dt):
    return 4 if dt == F32 else (2 if dt == BF16 else 1)


class Buf:
    def __init__(self, name):
        self.name = name
        self.w = None
        self.r = {}
        self.dsem = None
        self.dcnt = 0


class _Rec:
    def __init__(self):
        self.call = None

    def __getattr__(self, name):
        def f(*a, **k):
            self.call = (name, a, k)
            return self
        return f


class Prog:
    ENGS = ("pe", "act", "dve", "pool", "sp")

    def __init__(self, nc, stack):
        self.nc = nc
        self.stack = stack
        self.q = {e: [] for e in self.ENGS}
        self.sem = {e: stack.enter_context(nc.semaphore("s_" + e)) for e in ("pe", "act", "dve", "pool")}
        self.cnt = {e: 0 for e in ("pe", "act", "dve", "pool")}
        self.waited = {e: {} for e in self.ENGS}
        self.semof = {}
        self.dbufs = []
        self.nsem = 4

    def _dep(self, eng, dep, same_ok=True):
        sem, val = dep
        key = id(sem)
        own = self.sem.get(eng)
        if own is not None and sem is own:
            if eng == "pe":
                return
            if val > self.cnt[eng]:
                return
        if self.waited[eng].get(key, 0) >= val:
            return
        self.waited[eng][key] = val
        self.q[eng].append(("wait", sem, val))

    def op(self, eng, fn, reads=(), writes=(), signal=True):
        for b in reads:
            if b.w is not None:
                self._dep(eng, b.w)
        for b in writes:
            if b.w is not None:
                self._dep(eng, b.w)
            for k, d in b.r.items():
                if d[0] is self.sem.get(eng):
                    continue
                self._dep(eng, d)
        mysem = self.sem[eng]
        val = self.cnt[eng] + 1
        rec = _Rec()
        fn(rec)
        assert rec.call is not None
        self.q[eng].append(("inst", rec.call, mysem if signal else None))
        if signal:
            self.cnt[eng] = val
        for b in reads:
            b.r[id(mysem)] = (mysem, val)
        for b in writes:
            b.w = (mysem, val)
            b.r = {}

    def dma(self, out_ap, in_ap, sembuf, reads=(), writes=(), queue="sp"):
        eng = queue
        for b in reads:
            if b.w is not None:
                self._dep(eng, b.w)
        for b in writes:
            if b.w is not None:
                self._dep(eng, b.w)
            for k, d in b.r.items():
                self._dep(eng, d)
        if sembuf.dsem is None:
            sembuf.dsem = self.stack.enter_context(self.nc.semaphore("d_" + sembuf.name))
            self.nsem += 1
            self.dbufs.append(sembuf)
        sembuf.dcnt += 16
        self.q[eng].append(("dma", out_ap, in_ap, sembuf.dsem))
        d = (sembuf.dsem, sembuf.dcnt)
        for b in reads:
            b.r[id(sembuf.dsem)] = d
        for b in writes:
            b.w = d
            b.r = {}

    def barrier(self):
        deps = [(self.sem[e], self.cnt[e]) for e in ("pe", "act", "dve", "pool") if self.cnt[e] > 0]
        deps += [(b.dsem, b.dcnt) for b in self.dbufs]
        for e in self.ENGS:
            for d in deps:
                self._dep(e, d)

    def finish(self):
        nc = self.nc
        engobj = {"pe": "tensor", "act": "scalar", "dve": "vector", "pool": "gpsimd", "sp": "sync"}
        with nc.Block() as block:
            for e in self.ENGS:
                items = self.q[e]

                def body(eng, items=items):
                    for it in items:
                        if it[0] == "wait":
                            eng.wait_ge(it[1], it[2])
                        elif it[0] == "inst":
                            name, a, k = it[1]
                            r = getattr(eng, name)(*a, **k)
                            if it[2] is not None:
                                r.then_inc(it[2], 1)
                        else:
                            eng.dma_start(out=it[1], in_=it[2]).then_inc(it[3], 16)

                getattr(block, engobj[e])(body)


class Arena:
    def __init__(self, ap, size):
        self.ap = ap
        self.size = size
        self.off = 0

    def alloc(self, shape, dt, parts=128):
        n = int(np.prod(shape)) * _dsize(dt)
        off = (self.off + 63) // 64 * 64
        assert off + n <= self.size, ("arena overflow", off, n, self.size)
        a = self.ap[0:parts, off:off + n].bitcast(dt)
        if len(shape) > 1:
            names = ["d%d" % i for i in range(len(shape))]
            pat = "p (" + " ".join(names) + ") -> p " + " ".join(names)
            a = a.rearrange(pat, **{names[i]: int(shape[i]) for i in range(len(shape))})
        self.off = off + n
        return a


class Ring:
    def __init__(self, items):
        self.items = items
        self.i = 0

    def next(self):
        it = self.items[self.i % len(self.items)]
        self.i += 1
        return it


def mkring(ar, name, n, shape, dt, parts=128):
    return Ring([(ar.alloc(shape, dt, parts), Buf("%s%d" % (name, i))) for i in range(n)])


C_ID = 0
C_ML = 128
C_MR = 256
C_TRI = 384
C_IND = 896
C_SEL = 898
C_W = C_SEL + 8 * 128


def make_consts():
    c = np.zeros((128, C_W), np.float32)
    a = np.arange(128)
    c[:, C_ID:C_ID + 128] = np.eye(128, dtype=np.float32)
    c[:, C_ML:C_ML + 128] = (a[None, :] <= a[:, None]).astype(np.float32)
    c[:, C_MR:C_MR + 128] = (a[:, None] <= a[None, :]).astype(np.float32)
    same = (a[:, None] // 64) == (a[None, :] // 64)
    s_ = a[:, None]
    c_ = a[None, :]
    c[:, C_TRI + 0:C_TRI + 128] = (same & (s_ <= c_)).astype(np.float32)
    c[:, C_TRI + 128:C_TRI + 256] = (same & (s_ > c_)).astype(np.float32)
    c[:, C_TRI + 256:C_TRI + 384] = (same & (s_ >= c_)).astype(np.float32)
    c[:, C_TRI + 384:C_TRI + 512] = (same & (s_ < c_)).astype(np.float32)
    c[:, C_IND + 0] = (a < 64)
    c[:, C_IND + 1] = (a >= 64)
    inv = (10000.0 ** (-np.arange(16, dtype=np.float32) / 16.0)).astype(np.float32)
    tok = np.arange(T)
    rows = (tok // 64).astype(np.float32)
    cols = (tok % 64).astype(np.float32)
    ar = rows[:, None] * inv[None, :]
    ac = cols[:, None] * inv[None, :]
    cr, sr, cc, sc = np.cos(ar), np.sin(ar), np.cos(ac), np.sin(ac)
    Ct = np.concatenate([cr, cr, cc, cc], axis=1).astype(np.float32)
    St = np.concatenate([-sr, sr, -sc, sc], axis=1).astype(np.float32)
    rope = np.zeros((128, 2048), np.float32)
    rope[:, 0:1024] = Ct.reshape(16, 128, 64).transpose(1, 0, 2).reshape(128, 1024)
    rope[:, 1024:2048] = St.reshape(16, 128, 64).transpose(1, 0, 2).reshape(128, 1024)
    for r in range(8):
        c[r, C_SEL + r * 128:C_SEL + (r + 1) * 128] = 1.0
    return c, rope


PV_AQ, PV_AK, PV_BQ, PV_BK = 0, 64, 128, 192
PV_SINK = 256
PV_SUBLN = 264
PV_LQ1, PV_LK1, PV_LQ2, PV_LK2 = 392, 456, 520, 584
PV_ONORM = 648
PV_W = 904


def build(NB, stage=2):
    nc = bass.Bass("TRN2", target_bir_lowering=False)

    def din(name, shape, dt=F32):
        return nc.dram_tensor(name, list(shape), dt, kind="ExternalInput").ap()

    x_d = din("x", [NB, T, D])
    ctx_d = din("ctx", [NB, LC, D])
    cT_d = din("cT", [128, 8, 8])
    adw_d = din("adaln_w", [2, D, 3 * D])
    adb_d = din("adaln_b", [2, 3 * D])
    ngT_d = din("normgT", [2, 128, 8])
    wout_d = din("w_out", [2, D, D])
    win0_d = din("ab_w_in", [D, E_IN])
    win1_d = din("gla_w_in", [D, O_IN])
    waf_d = din("gla_wa_f", [16, 512])
    wab_d = din("gla_wa_b", [16, 512])
    pv_d = din("pvec", [PV_W])
    baf_d = din("gla_ba_f", [1, 512])
    bab_d = din("gla_ba_b", [1, 512])
    cst_d = din("consts", [128, C_W])
    rope_d = din("rope", [128, 2048])
    out_d = nc.dram_tensor("out", [NB, T, D], F32, kind="ExternalOutput").ap()
    ctx1_kind = "ExternalOutput" if stage == 1 else "Internal"
    ctx1_d = nc.dram_tensor("ctx1", [NB, LC, D], F32, kind=ctx1_kind).ap()
    pj_d = nc.dram_tensor("pj", [NB, NT * 128, E_IN], BF16, kind="Internal").ap()

    with ExitStack() as st:
        P = Prog(nc, st)
        arena_t = st.enter_context(nc.sbuf_tensor("arena", [128, ARENA_BYTES], U8))
        psum_t = st.enter_context(nc.psum_tensor("psum", [128, 8, 512], F32))
        AR = Arena(arena_t, ARENA_BYTES)
        banks = [(psum_t[:, i, :], Buf("bank%d" % i)) for i in range(8)]

        cst = AR.alloc([C_W], F32)
        b_cst = Buf("cst")
        pv = AR.alloc([PV_W], F32)
        b_pv = Buf("pv")
        identb = AR.alloc([128], BF16)
        maskb = AR.alloc([2, 4, 128], BF16)
        b_cb = Buf("constb")
        nhalf = AR.alloc([32], F32)
        ones_f = AR.alloc([8], F32)
        scT = AR.alloc([8, 8], F32)
        b_scT = Buf("scT")
        msbg = AR.alloc([2, D], F32)
        b_msb = [Buf("msb0"), Buf("msb1")]
        scl = AR.alloc([2, 8, 8], F32)
        shf = AR.alloc([2, 8, 8], F32)
        b_mod = [Buf("mod0"), Buf("mod1")]
        ngT = AR.alloc([2, 8], F32)
        b_ngT = Buf("ngT")
        gain_all = AR.alloc([26, 64], F32)
        esink = AR.alloc([8], F32)
        neglam = AR.alloc([1], F32)
        subln8 = AR.alloc([128], F32)
        b_par = Buf("par")
        persist_mark = AR.off

        def V(fn):
            return fn

        P.dma(cst, cst_d, b_cst, writes=[b_cst])
        P.dma(pv, pv_d.partition_broadcast(128), b_pv, writes=[b_pv])
        P.dma(ngT, ngT_d.rearrange("l p k -> p l k"), b_ngT, writes=[b_ngT])
        P.dma(scT, cT_d, b_scT, writes=[b_scT])

        P.op("pool", lambda e: e.memset(nhalf, -0.5), writes=[b_cb])
        P.op("pool", lambda e: e.memset(ones_f, 1.0), writes=[b_cb])
        P.op("dve", lambda e: e.tensor_copy(out=identb, in_=cst[:, C_ID:C_ID + 128]), reads=[b_cst], writes=[b_cb])
        P.op("dve", lambda e: e.tensor_copy(
            out=maskb[:, 0], in_=cst[:, C_ML:C_ML + 128].unsqueeze(1).broadcast_to([128, 4, 128])),
            reads=[b_cst], writes=[b_cb])
        P.op("dve", lambda e: e.tensor_copy(
            out=maskb[:, 1], in_=cst[:, C_MR:C_MR + 128].unsqueeze(1).broadcast_to([128, 4, 128])),
            reads=[b_cst], writes=[b_cb])
        for (h0, n, off) in ((0, 8, PV_AQ), (8, 2, PV_AK), (10, 8, PV_BQ), (18, 8, PV_BK)):
            P.op("dve", lambda e, h0=h0, n=n, off=off: e.tensor_copy(
                out=gain_all[:, h0:h0 + n, :], in_=pv[:, off:off + 64].unsqueeze(1).broadcast_to([128, n, 64])),
                reads=[b_pv], writes=[b_par])
        P.op("act", lambda e: e.activation(out=esink, in_=pv[:, PV_SINK:PV_SINK + 8], func=AF.Exp),
             reads=[b_pv], writes=[b_par])
        ltmp = AR.alloc([64], F32)
        lred = AR.alloc([2], F32)
        lexp = AR.alloc([2], F32)
        b_l = Buf("ltmp")
        for i, (oa, ob) in enumerate(((PV_LQ1, PV_LK1), (PV_LQ2, PV_LK2))):
            P.op("dve", lambda e, oa=oa, ob=ob: e.tensor_tensor(
                out=ltmp, in0=pv[:, oa:oa + 64], in1=pv[:, ob:ob + 64], op=ALU.mult), reads=[b_pv], writes=[b_l])
            P.op("dve", lambda e, i=i: e.tensor_reduce(out=lred[:, i:i + 1], in_=ltmp, axis=AXX, op=ALU.add),
                 reads=[b_l], writes=[b_l])
        P.op("act", lambda e: e.activation(out=lexp, in_=lred, func=AF.Exp), reads=[b_l], writes=[b_l])
        lam_init0 = 0.8 - 0.6 * math.exp(-0.3 * 0)
        P.op("dve", lambda e: e.tensor_tensor(out=neglam, in0=lexp[:, 1:2], in1=lexp[:, 0:1], op=ALU.subtract),
             reads=[b_l], writes=[b_par])
        P.op("dve", lambda e: e.tensor_scalar(out=neglam, in0=neglam, scalar1=-lam_init0, scalar2=None, op0=ALU.add),
             reads=[b_par], writes=[b_par])
        P.op("dve", lambda e: e.tensor_scalar(out=subln8, in0=pv[:, PV_SUBLN:PV_SUBLN + 128],
                                              scalar1=1.0 - lam_init0, scalar2=None, op0=ALU.mult),
             reads=[b_pv], writes=[b_par])
        P.op("act", lambda e: e.activation(out=scT, in_=scT, func=AF.Silu), reads=[b_scT], writes=[b_scT])

        setup_mark = AR.off
        wst = mkring(AR, "wst", 2, [3 * D], F32)
        msb = AR.alloc([3 * D], F32)
        b_msbt = Buf("msbt")
        brow = AR.alloc([3 * D], F32, parts=1)
        b_brow = Buf("brow")
        for l in range(2):
            P.dma(brow, adb_d[l:l + 1, :], b_brow, writes=[b_brow])
            for kc in range(8):
                wt, bw = wst.next()
                P.dma(wt, adw_d[l, kc * 128:(kc + 1) * 128, :], bw, writes=[bw])
                for g in range(6):
                    bk, bb = banks[g]
                    P.op("pe", lambda e, bk=bk, kc=kc, wt=wt, g=g: e.matmul(
                        bk[0:8, :], lhsT=scT[:, kc, :], rhs=wt[:, g * 512:(g + 1) * 512],
                        start=(kc == 0), stop=False), reads=[b_scT, bw], writes=[bb], signal=(g == 5))
            for g in range(6):
                bk, bb = banks[g]
                P.op("pe", lambda e, bk=bk, g=g: e.matmul(
                    bk[0:8, :], lhsT=ones_f[0:1, :], rhs=brow[0:1, g * 512:(g + 1) * 512],
                    start=False, stop=True), reads=[b_cb, b_brow], writes=[bb])
                P.op("dve" if g % 2 else "act",
                     (lambda e, bk=bk, g=g, l=l: e.tensor_copy(out=msb[0:8, g * 512:(g + 1) * 512], in_=bk[0:8, :]))
                     if g % 2 else
                     (lambda e, bk=bk, g=g, l=l: e.activation(out=msb[0:8, g * 512:(g + 1) * 512], in_=bk[0:8, :],
                                                              func=AF.Copy)),
                     reads=[bb], writes=[b_msbt])
            P.op("dve", lambda e, l=l: e.tensor_copy(out=msbg[0:8, l, :], in_=msb[0:8, 2048:3072]),
                 reads=[b_msbt], writes=[b_msb[l]])
            bk, bb = banks[6]
            tpv = bk[:, 0:128].rearrange("p (c r) -> p c r", c=16)
            for c in range(16):
                P.op("pe", lambda e, c=c, l=l, tpv=tpv: e.transpose(
                    out=tpv[:, c, :], in_=msb[0:8, c * 128:(c + 1) * 128], identity=cst[0:8, C_ID:C_ID + 8]),
                    reads=[b_msbt, b_cst], writes=[bb], signal=(c == 15))
            P.op("dve", lambda e, l=l, tpv=tpv: e.tensor_copy(out=shf[:, l], in_=tpv[:, 0:8, :]),
                 reads=[bb], writes=[b_mod[l]])
            P.op("dve", lambda e, l=l, tpv=tpv: e.tensor_scalar(
                out=scl[:, l], in0=tpv[:, 8:16, :], scalar1=1.0, scalar2=None, op0=ALU.add),
                reads=[bb], writes=[b_mod[l]])
            P.op("dve", lambda e, l=l: e.tensor_tensor(
                out=scl[:, l], in0=scl[:, l], in1=ngT[:, l, :].unsqueeze(2).broadcast_to([128, 8, 8]), op=ALU.mult),
                reads=[b_mod[l], b_ngT], writes=[b_mod[l]])
        P.barrier()
        AR.off = setup_mark

        l0_mark = AR.off
        Wb = AR.alloc([8, E_IN], BF16)
        b_Wb = Buf("Wb")
        blocks = ((0, 512, 0), (512, 128, 512), (1280, 512, 640), (1792, 512, 1152),
                  (2304, 512, 1664), (640, 128, 2176), (768, 512, 2304), (2816, 512, 2816))
        w0v = win0_d.rearrange("(kc p) n -> p kc n", p=128)
        for (src, n, dst) in blocks:
            P.dma(Wb[:, :, dst:dst + n], w0v[:, :, src:src + n], b_Wb, writes=[b_Wb], queue="pool")
        colgroups = ((0, 512), (512, 512), (1024, 512), (1536, 128), (1664, 512), (2176, 128),
                     (2304, 512), (2816, 512))

        xt_r = mkring(AR, "xt", 2, [D], F32)
        junk = AR.alloc([D], BF16)
        xn_r = mkring(AR, "xn", 2, [D], BF16)
        hT_r = mkring(AR, "hT", 2, [8, 128], BF16)
        qk_r = mkring(AR, "qk", 2, [26, 64], F32)
        tA_r = mkring(AR, "tA", 2, [26, 64], F32)
        tB_r = mkring(AR, "tB", 2, [26, 64], F32)
        tC_r = mkring(AR, "tC", 2, [26, 64], F32)
        po_r = mkring(AR, "po", 2, [E_IN], BF16)
        st_r = mkring(AR, "stat", 4, [64], F32)
        pbank = Ring([banks[0], banks[1]])
        tbank = Ring([banks[2], banks[3]])
        b_pj = [[Buf("pj%d_%d" % (b, t)) for t in range(NT)] for b in range(NB)]
        ropet = AR.alloc([2, 16, 64], F32)
        b_rope = Buf("rope")
        P.dma(ropet, rope_d.rearrange("p (a t d) -> p a t d", a=2, t=16), b_rope, writes=[b_rope])
        ropeC = ropet[:, 0]
        ropeS = ropet[:, 1]

        def emit_hT(src_ap, l, r, b_src_dram=None):
            xt, bx = xt_r.next()
            P.dma(xt, src_ap, bx, reads=([b_src_dram] if b_src_dram else []), writes=[bx])
            stt, bs = st_r.next()
            P.op("act", lambda e: e.activation(out=junk, in_=xt, func=AF.Square, accum_out=stt[:, 0:1]),
                 reads=[bx], writes=[bs])
            P.op("dve", lambda e: e.tensor_scalar(out=stt[:, 1:2], in0=stt[:, 0:1], scalar1=1.0 / D, scalar2=EPS,
                                                  op0=ALU.mult, op1=ALU.add), reads=[bs], writes=[bs])
            P.op("pool", lambda e: e.tensor_tensor(out=stt[:, 2:3], in0=stt[:, 1:2], in1=nhalf[:, 0:1], op=ALU.pow),
                 reads=[bs, b_cb], writes=[bs])
            xn, bn = xn_r.next()
            P.op("act", lambda e: e.activation(out=xn, in_=xt, func=AF.Copy, scale=stt[:, 2:3]),
                 reads=[bx, bs], writes=[bn])
            tb, btb = tbank.next()
            tpb = tb.bitcast(BF16).rearrange("p (k t) -> p k t", k=8)
            for kc in range(8):
                P.op("pe", lambda e, kc=kc: e.transpose(out=tpb[:, kc, :], in_=xn[:, kc * 128:(kc + 1) * 128],
                                                        identity=identb),
                     reads=[bn, b_cb], writes=[btb], signal=(kc == 7))
            hT, bh = hT_r.next()
            for kc in range(8):
                if kc % 2 == 0:
                    P.op("dve", lambda e, kc=kc: e.tensor_scalar(
                        out=hT[:, kc, :], in0=tpb[:, kc, :], scalar1=scl[:, l, kc, r:r + 1],
                        scalar2=shf[:, l, kc, r:r + 1], op0=ALU.mult, op1=ALU.add),
                        reads=[btb, b_mod[l]], writes=[bh])
                else:
                    P.op("act", lambda e, kc=kc: e.activation(
                        out=hT[:, kc, :], in_=tpb[:, kc, :], func=AF.Identity,
                        bias=shf[:, l, kc, r:r + 1], scale=scl[:, l, kc, r:r + 1]),
                        reads=[btb, b_mod[l]], writes=[bh])
            return hT, bh, xt, bx

        def phaseP0(b):
            for t in range(NT):
                isx = t < NTX
                src = x_d[b, t * 128:(t + 1) * 128, :] if isx else ctx_d[b, (t - NTX) * 128:(t - NTX + 1) * 128, :]
                hT, bh, _, _ = emit_hT(src, 0, b if isx else NB)
                qk, bqk = qk_r.next()
                qkf = qk.rearrange("p h d -> p (h d)")
                po, bpo = po_r.next()
                for gi, (c0, n) in enumerate(colgroups):
                    pb, bpb = pbank.next()
                    for kc in range(8):
                        P.op("pe", lambda e, pb=pb, kc=kc, c0=c0, n=n: e.matmul(
                            pb[:, 0:n], lhsT=hT[:, kc, :], rhs=Wb[:, kc, c0:c0 + n], start=(kc == 0), stop=(kc == 7)),
                            reads=[bh, b_Wb], writes=[bpb], signal=(kc == 7))
                    if gi < 4:
                        eng = "act" if gi % 2 == 0 else "dve"
                        if eng == "act":
                            P.op("act", lambda e, pb=pb, c0=c0, n=n: e.activation(out=qkf[:, c0:c0 + n], in_=pb[:, 0:n],
                                                                                  func=AF.Copy),
                                 reads=[bpb], writes=[bqk])
                        else:
                            P.op("dve", lambda e, pb=pb, c0=c0, n=n: e.tensor_copy(out=qkf[:, c0:c0 + n], in_=pb[:, 0:n]),
                                 reads=[bpb], writes=[bqk])
                    elif gi < 6:
                        P.op("dve", lambda e, pb=pb, c0=c0, n=n: e.tensor_copy(out=po[:, c0:c0 + n], in_=pb[:, 0:n]),
                             reads=[bpb], writes=[bpo])
                    else:
                        P.op("act", lambda e, pb=pb, c0=c0, n=n: e.activation(out=po[:, c0:c0 + n], in_=pb[:, 0:n],
                                                                              func=AF.Silu),
                             reads=[bpb], writes=[bpo])
                tA, bA = tA_r.next()
                tB, bB = tB_r.next()
                stt, bs = st_r.next()
                P.op("pool", lambda e: e.tensor_tensor(out=tA, in0=qk, in1=qk, op=ALU.mult), reads=[bqk], writes=[bA])
                P.op("dve", lambda e: e.tensor_reduce(out=stt[:, 0:26], in_=tA, axis=AXX, op=ALU.add),
                     reads=[bA], writes=[bs])
                P.op("dve", lambda e: e.tensor_scalar(out=stt[:, 0:26], in0=stt[:, 0:26], scalar1=1.0 / 64, scalar2=EPS,
                                                      op0=ALU.mult, op1=ALU.add), reads=[bs], writes=[bs])
                P.op("pool", lambda e: e.tensor_tensor(out=stt[:, 32:58], in0=stt[:, 0:26], in1=nhalf[:, 0:26],
                                                       op=ALU.pow), reads=[bs, b_cb], writes=[bs])
                P.op("dve", lambda e: e.tensor_tensor(
                    out=tB, in0=qk, in1=stt[:, 32:58].unsqueeze(2).broadcast_to([128, 26, 64]), op=ALU.mult),
                    reads=[bqk, bs], writes=[bB])
                pov = po[:, 0:1664].rearrange("p (h d) -> p h d", h=26)
                if isx:
                    P.op("pool", lambda e: e.tensor_tensor(out=tA, in0=tB, in1=gain_all, op=ALU.mult),
                         reads=[bB, b_par], writes=[bA])
                    tC, bC = tC_r.next()
                    P.op("pool", lambda e: e.tensor_tensor(
                        out=tB, in0=tA, in1=ropeC[:, t, :].unsqueeze(1).broadcast_to([128, 26, 64]), op=ALU.mult),
                        reads=[bA, b_rope], writes=[bB])
                    for blk in range(2):
                        for hf in range(2):
                            o0 = blk * 32 + hf * 16
                            i0 = blk * 32 + (1 - hf) * 16
                            P.op("dve", lambda e, o0=o0, i0=i0: e.tensor_tensor(
                                out=tC[:, :, o0:o0 + 16], in0=tA[:, :, i0:i0 + 16],
                                in1=ropeS[:, t, o0:o0 + 16].unsqueeze(1).broadcast_to([128, 26, 16]), op=ALU.mult),
                                reads=[bA, b_rope], writes=[bC])
                    P.op("dve", lambda e: e.tensor_tensor(out=pov, in0=tB, in1=tC, op=ALU.add),
                         reads=[bB, bC], writes=[bpo])
                else:
                    P.op("pool", lambda e: e.tensor_tensor(out=pov, in0=tB, in1=gain_all, op=ALU.mult),
                         reads=[bB, b_par], writes=[bpo])
                P.dma(pj_d[b, t * 128:(t + 1) * 128, :], po, bpo, reads=[bpo], writes=[b_pj[b][t]])

        for b in range(NB):
            phaseP0(b)
        P.barrier()
        AR.off = l0_mark

        Wob = AR.alloc([8, D], BF16)
        b_Wob = Buf("Wob")
        P.dma(Wob, wout_d[0].rearrange("(kc p) n -> p kc n", p=128), b_Wob, writes=[b_Wob], queue="pool")
        b_kraw = Buf("kraw")
        kAT = AR.alloc([2, NT * 128], BF16)
        kBT = AR.alloc([4, NT * 128], BF16)
        b_kT = Buf("kT")
        vA = AR.alloc([NT, 2, 66], BF16)
        vB = AR.alloc([NT, 4, 130], BF16)
        b_vA = Buf("vA")
        b_vB = Buf("vB")
        gbc = AR.alloc([2, D], F32)
        b_gbc = Buf("gbc")
        qg_r = mkring(AR, "qg", 2, [2, 2048], BF16)
        qAT_r = mkring(AR, "qAT", 2, [8, 256], BF16)
        qBT_r = mkring(AR, "qBT", 2, [4, 256], BF16)
        _m = AR.off
        kraw = AR.alloc([NT, 640], BF16)
        AR.off = _m
        PA_r = Ring([(AR.alloc([5, 512], BF16), [Buf("PA%d_%d" % (i, k)) for k in range(5)]) for i in range(2)])
        PB_r = Ring([(AR.alloc([NT, 256], BF16), [Buf("PB%d_%d" % (i, k)) for k in range(9)]) for i in range(2)])
        pall = [bb_ for (_, bl) in PA_r.items for bb_ in bl] + [bb_ for (_, bl) in PB_r.items for bb_ in bl]
        otok_r = mkring(AR, "otok", 2, [D], BF16)
        dB_r = mkring(AR, "dB", 2, [4, 128], F32)
        oT_r = mkring(AR, "oT", 2, [8, 128], BF16)
        xr_r = mkring(AR, "xr", 2, [D], F32)
        ty_r = mkring(AR, "ty", 2, [512], F32)
        tmp_r = mkring(AR, "tmp", 3, [512], F32)
        s2_r = mkring(AR, "s2", 4, [32], F32)
        sbank = Ring([banks[0], banks[1], banks[2]])
        pvbank_A = Ring([banks[3], banks[4]])
        T0, T1, YB = banks[5], banks[6], banks[7]
        b_x1 = [[Buf("x1_%d_%d" % (b, t)) for t in range(NT)] for b in range(NB)]
        pjv = [pj_d[b].rearrange("(t p) c -> p t c", p=128) for b in range(NB)]

        def phaseATT0(b):
            for ri, r in enumerate((b, NB)):
                for cg in range(2):
                    yb, byb = YB
                    P.op("pe", lambda e, r=r, cg=cg, yb=yb: e.matmul(
                        yb, lhsT=cst[0:8, C_SEL + r * 128:C_SEL + (r + 1) * 128],
                        rhs=msbg[0:8, 0, cg * 512:(cg + 1) * 512], start=True, stop=True),
                        reads=[b_cst, b_msb[0]], writes=[byb])
                    P.op("dve", lambda e, ri=ri, cg=cg, yb=yb: e.tensor_copy(out=gbc[:, ri, cg * 512:(cg + 1) * 512], in_=yb),
                         reads=[byb], writes=[b_gbc])
            P.dma(kraw[:, :, 0:128], pjv[b][:, :, 512:640], b_kraw, reads=b_pj[b], writes=[b_kraw] + pall)
            P.dma(kraw[:, :, 128:640], pjv[b][:, :, 1152:1664], b_kraw, reads=b_pj[b], writes=[b_kraw] + pall)
            for t in range(NT):
                P.dma(vB[:, t, :, 0:128], pjv[b][:, t, 1664:2176].rearrange("p (h d) -> p h d", h=4),
                      b_vB, reads=[b_pj[b][t]], writes=[b_vB])
                P.dma(vA[:, t, :, 0:64], pjv[b][:, t, 2176:2304].rearrange("p (h d) -> p h d", h=2),
                      b_vA, reads=[b_pj[b][t]], writes=[b_vA])
            P.op("pool", lambda e: e.memset(vA[:, :, :, 64:66], 1.0), writes=[b_vA])
            P.op("pool", lambda e: e.memset(vB[:, :, :, 128:130], 1.0), writes=[b_vB])
            for t in range(NT):
                tb, btb = T0 if t % 2 == 0 else T1
                tv = tb.bitcast(BF16)
                ta = tv[0:64, 0:256].rearrange("p (h t) -> p h t", h=2)
                tbb = tv[:, 256:768].rearrange("p (h t) -> p h t", h=4)
                for kv in range(2):
                    P.op("pe", lambda e, kv=kv, ta=ta: e.transpose(out=ta[:, kv, :], in_=kraw[:, t, kv * 64:(kv + 1) * 64],
                                                                   identity=identb),
                         reads=[b_kraw, b_cb], writes=[btb], signal=False)
                for h in range(4):
                    P.op("pe", lambda e, h=h, tbb=tbb: e.transpose(
                        out=tbb[:, h, :], in_=kraw[:, t, 128 + h * 128:128 + (h + 1) * 128], identity=identb),
                        reads=[b_kraw, b_cb], writes=[btb], signal=(h == 3))
                P.op("act", lambda e, ta=ta: e.activation(out=kAT[0:64, :, t * 128:(t + 1) * 128], in_=ta, func=AF.Copy),
                     reads=[btb], writes=[b_kT])
                P.op("dve", lambda e, tbb=tbb: e.tensor_copy(out=kBT[:, :, t * 128:(t + 1) * 128], in_=tbb),
                     reads=[btb], writes=[b_kT])

            for G in range(9):
                isx = G < 8
                tiles = (2 * G, 2 * G + 1)
                ri = 0 if isx else 1
                qg, bqg = qg_r.next()
                tsl = slice(2 * G, 2 * G + 2)
                rd = [b_pj[b][tiles[0]], b_pj[b][tiles[1]]]
                P.dma(qg[:, :, 0:512], pjv[b][:, tsl, 0:512], bqg, reads=rd, writes=[bqg])
                P.dma(qg[:, :, 512:1024], pjv[b][:, tsl, 640:1152], bqg, reads=rd, writes=[bqg])
                P.dma(qg[:, :, 1024:2048], pjv[b][:, tsl, 2304:3328], bqg, reads=rd, writes=[bqg])
                qAT, bqa = qAT_r.next()
                qBT, bqb = qBT_r.next()
                for j in range(2):
                    tb0, bt0 = T0
                    tb1, bt1 = T1
                    ta = tb0.bitcast(BF16)[0:64, :].rearrange("p (h t) -> p h t", h=8)
                    tbb = tb1.bitcast(BF16)[:, 0:512].rearrange("p (h t) -> p h t", h=4)
                    for h in range(8):
                        P.op("pe", lambda e, h=h, j=j, ta=ta: e.transpose(
                            out=ta[:, h, :], in_=qg[:, j, h * 64:(h + 1) * 64], identity=identb),
                            reads=[bqg, b_cb], writes=[bt0], signal=(h == 7))
                    for h in range(4):
                        P.op("pe", lambda e, h=h, j=j, tbb=tbb: e.transpose(
                            out=tbb[:, h, :], in_=qg[:, j, 512 + h * 128:512 + (h + 1) * 128], identity=identb),
                            reads=[bqg, b_cb], writes=[bt1], signal=(h == 3))
                    P.op("act", lambda e, j=j, ta=ta: e.activation(out=qAT[0:64, :, j * 128:(j + 1) * 128], in_=ta,
                                                                   func=AF.Copy), reads=[bt0], writes=[bqa])
                    P.op("dve", lambda e, j=j, tbb=tbb: e.tensor_copy(out=qBT[:, :, j * 128:(j + 1) * 128], in_=tbb),
                         reads=[bt1], writes=[bqb])
                otoks = [otok_r.next() for _ in range(2)]
                for j in range(2):
                    ti = tiles[j] if isx else NTX + j
                    if isx:
                        kts = []
                        if ti > 0:
                            kts.append((ti - 1, 0))
                        kts.append((ti, None))
                        if ti < NTX - 1:
                            kts.append((ti + 1, 1))
                        kts += [(NTX, None), (NTX + 1, None)]
                    else:
                        kts = [(NTX, None), (NTX + 1, None)]
                    otok, bot = otoks[j]
                    for kv in range(2):
                        PA, bPA = PA_r.next()
                        for idx, (kt, mk) in enumerate(kts):
                            sb, bsb = sbank.next()
                            P.op("pe", lambda e, sb=sb, kv=kv, kt=kt, j=j: e.matmul(
                                sb, lhsT=kAT[0:64, kv, kt * 128:(kt + 1) * 128],
                                rhs=qAT[0:64, kv * 4:(kv + 1) * 4, j * 128:(j + 1) * 128], start=True, stop=True),
                                reads=[b_kT, bqa], writes=[bsb])
                            P.op("act", lambda e, sb=sb, idx=idx, PA=PA: e.activation(
                                out=PA[:, idx, :], in_=sb, func=AF.Exp, scale=0.125), reads=[bsb], writes=[bPA[idx]])
                            if mk is not None:
                                P.op("pool", lambda e, idx=idx, PA=PA, mk=mk: e.tensor_tensor(
                                    out=PA[:, idx, :], in0=PA[:, idx, :],
                                    in1=maskb[:, mk].rearrange("p h q -> p (h q)"), op=ALU.mult),
                                    reads=[bPA[idx], b_cb], writes=[bPA[idx]])
                        pvb, bpv = pvbank_A.next()
                        pvv = pvb[:, 0:4 * 66].rearrange("p (g d) -> p g d", g=4)
                        for g in range(4):
                            for idx, (kt, mk) in enumerate(kts):
                                P.op("pe", lambda e, g=g, idx=idx, kt=kt, kv=kv, PA=PA, pvv=pvv: e.matmul(
                                    pvv[:, g, 0:65], lhsT=PA[:, idx, g * 128:(g + 1) * 128], rhs=vA[:, kt, kv, 0:65],
                                    start=(idx == 0), stop=(idx == len(kts) - 1)),
                                    reads=[bPA[idx], b_vA], writes=[bpv], signal=(idx == len(kts) - 1))
                        s2, bs2 = s2_r.next()
                        tmp, btm = tmp_r.next()
                        P.op("dve", lambda e, pvv=pvv, s2=s2, kv=kv: e.tensor_tensor(
                            out=s2[:, 0:4], in0=pvv[:, :, 64], in1=esink[:, kv * 4:(kv + 1) * 4], op=ALU.add),
                            reads=[bpv, b_par], writes=[bs2])
                        P.op("dve", lambda e, s2=s2: e.reciprocal(out=s2[:, 4:8], in_=s2[:, 0:4]), reads=[bs2], writes=[bs2])
                        tmpv = tmp[:, 0:256].rearrange("p (g d) -> p g d", g=4)
                        P.op("dve", lambda e, pvv=pvv, s2=s2, tmpv=tmpv: e.tensor_tensor(
                            out=tmpv, in0=pvv[:, :, 0:64], in1=s2[:, 4:8].unsqueeze(2).broadcast_to([128, 4, 64]),
                            op=ALU.mult), reads=[bpv, bs2], writes=[btm])
                        P.op("pool", lambda e, tmp=tmp, otok=otok, kv=kv, j=j: e.tensor_tensor(
                            out=otok[:, kv * 256:(kv + 1) * 256], in0=tmp[:, 0:256],
                            in1=qg[:, j, 1024 + kv * 256:1024 + (kv + 1) * 256], op=ALU.mult),
                            reads=[btm, bqg], writes=[bot])
                pairs = [(2 * i, 2 * i + 1) for i in range(9)] if isx else [(NTX, NTX + 1)]
                dBs = [dB_r.next() for _ in range(2)]
                for h in range(4):
                    pvj = [banks[3], banks[4]]
                    for s in range(2):
                        PB, bPB = PB_r.next()
                        for pi, pr in enumerate(pairs):
                            sb, bsb = sbank.next()
                            sbv = sb.rearrange("p (u q) -> p u q", u=2)
                            for u in range(2):
                                kt = pr[u]
                                P.op("pe", lambda e, sbv=sbv, u=u, kt=kt, s=s, h=h: e.matmul(
                                    sbv[:, u, :], lhsT=kBT[s * 64:(s + 1) * 64, h, kt * 128:(kt + 1) * 128],
                                    rhs=qBT[s * 64:(s + 1) * 64, h, :], start=True, stop=True),
                                    reads=[b_kT, bqb], writes=[bsb], signal=(u == 1))
                            P.op("act", lambda e, sbv=sbv, pi=pi, PB=PB: e.activation(
                                out=PB[:, 2 * pi:2 * pi + 2, :], in_=sbv, func=AF.Exp, scale=0.125),
                                reads=[bsb], writes=[bPB[pi]])
                        for j in range(2):
                            pvb, bpv = pvj[j]
                            pvv = pvb[:, 0:260].rearrange("p (s d) -> p s d", s=2)
                            nk = len(pairs) * 2
                            for pi, pr in enumerate(pairs):
                                for u in range(2):
                                    kt = pr[u]
                                    ki = 2 * pi + u
                                    P.op("pe", lambda e, pvv=pvv, s=s, ki=ki, kt=kt, j=j, PB=PB, h=h: e.matmul(
                                        pvv[:, s, 0:129], lhsT=PB[:, ki, j * 128:(j + 1) * 128], rhs=vB[:, kt, h, 0:129],
                                        start=(ki == 0), stop=(ki == nk - 1)),
                                        reads=[bPB[pi], b_vB], writes=[bpv], signal=(ki == nk - 1))
                    for j in range(2):
                        pvb, bpv = pvj[j]
                        pvv = pvb[:, 0:260].rearrange("p (s d) -> p s d", s=2)
                        dB, bdB = dBs[j]
                        s2, bs2 = s2_r.next()
                        tmp, btm = tmp_r.next()
                        P.op("dve", lambda e, pvv=pvv, s2=s2: e.reciprocal(out=s2[:, 0:2], in_=pvv[:, :, 128]),
                             reads=[bpv], writes=[bs2])
                        P.op("dve", lambda e, s2=s2: e.tensor_tensor(out=s2[:, 2:3], in0=s2[:, 1:2], in1=neglam, op=ALU.mult),
                             reads=[bs2, b_par], writes=[bs2])
                        P.op("dve", lambda e, pvv=pvv, s2=s2, tmp=tmp: e.tensor_scalar(
                            out=tmp[:, 0:128], in0=pvv[:, 1, 0:128], scalar1=s2[:, 2:3], scalar2=None, op0=ALU.mult),
                            reads=[bpv, bs2], writes=[btm])
                        P.op("dve", lambda e, pvv=pvv, s2=s2, tmp=tmp, dB=dB, h=h: e.scalar_tensor_tensor(
                            out=dB[:, h, :], in0=pvv[:, 0, 0:128], scalar=s2[:, 0:1], in1=tmp[:, 0:128],
                            op0=ALU.mult, op1=ALU.add), reads=[bpv, bs2, btm], writes=[bdB])
                for j in range(2):
                    ti = tiles[j] if isx else NTX + j
                    otok, bot = otoks[j]
                    dB, bdB = dBs[j]
                    s2, bs2 = s2_r.next()
                    tmp, btm = tmp_r.next()
                    tmp2, btm2 = tmp_r.next()
                    tv = tmp.rearrange("p (h d) -> p h d", h=4)
                    tv2 = tmp2.rearrange("p (h d) -> p h d", h=4)
                    P.op("pool", lambda e, tv=tv, dB=dB: e.tensor_tensor(out=tv, in0=dB, in1=dB, op=ALU.mult),
                         reads=[bdB], writes=[btm])
                    P.op("dve", lambda e, tv=tv, s2=s2: e.tensor_reduce(out=s2[:, 0:4], in_=tv, axis=AXX, op=ALU.add),
                         reads=[btm], writes=[bs2])
                    P.op("dve", lambda e, s2=s2: e.tensor_scalar(out=s2[:, 0:4], in0=s2[:, 0:4], scalar1=1.0 / 128,
                                                                 scalar2=EPS, op0=ALU.mult, op1=ALU.add),
                         reads=[bs2], writes=[bs2])
                    P.op("pool", lambda e, s2=s2: e.tensor_tensor(out=s2[:, 4:8], in0=s2[:, 0:4], in1=nhalf[:, 0:4], op=ALU.pow),
                         reads=[bs2, b_cb], writes=[bs2])
                    P.op("dve", lambda e, tv=tv, dB=dB, s2=s2: e.tensor_tensor(
                        out=tv, in0=dB, in1=s2[:, 4:8].unsqueeze(2).broadcast_to([128, 4, 128]), op=ALU.mult),
                        reads=[bdB, bs2], writes=[btm])
                    P.op("pool", lambda e, tv=tv, tv2=tv2: e.tensor_tensor(
                        out=tv2, in0=tv, in1=subln8.unsqueeze(1).broadcast_to([128, 4, 128]), op=ALU.mult),
                        reads=[btm, b_par], writes=[btm2])
                    P.op("pool", lambda e, tmp2=tmp2, otok=otok, j=j: e.tensor_tensor(
                        out=otok[:, 512:1024], in0=tmp2, in1=qg[:, j, 1536:2048], op=ALU.mult),
                        reads=[btm2, bqg], writes=[bot])
                    emit_outproj(b, ti, otok, bot, ri, isx, 0)

        def emit_outproj(b, ti, otok, bot, ri, isx, l):
            tb0, bt0 = T0
            tp = tb0.bitcast(BF16).rearrange("p (k t) -> p k t", k=8)
            for kc in range(8):
                P.op("pe", lambda e, kc=kc: e.transpose(out=tp[:, kc, :], in_=otok[:, kc * 128:(kc + 1) * 128],
                                                        identity=identb),
                     reads=[bot, b_cb], writes=[bt0], signal=(kc == 7))
            oT, boT = oT_r.next()
            P.op("act", lambda e: e.activation(out=oT, in_=tp, func=AF.Copy), reads=[bt0], writes=[boT])
            xr, bxr = xr_r.next()
            if l == 0:
                src = x_d[b, ti * 128:(ti + 1) * 128, :] if isx else ctx_d[b, (ti - NTX) * 128:(ti - NTX + 1) * 128, :]
                P.dma(xr, src, bxr, writes=[bxr])
            else:
                P.dma(xr, out_d[b, ti * 128:(ti + 1) * 128, :], bxr, reads=[b_x1[b][ti]], writes=[bxr])
            xo, bxo = xr, bxr
            for cg in range(2):
                yb, byb = YB
                for kc in range(8):
                    P.op("pe", lambda e, kc=kc, cg=cg, yb=yb: e.matmul(
                        yb, lhsT=oT[:, kc, :], rhs=Wob[:, kc, cg * 512:(cg + 1) * 512], start=(kc == 0), stop=(kc == 7)),
                        reads=[boT, b_Wob], writes=[byb], signal=(kc == 7))
                ty, bty = ty_r.next()
                P.op("dve", lambda e, yb=yb, ty=ty, cg=cg: e.tensor_tensor(
                    out=ty, in0=yb, in1=gbc[:, ri, cg * 512:(cg + 1) * 512], op=ALU.mult),
                    reads=[byb, b_gbc], writes=[bty])
                P.op("pool", lambda e, ty=ty, cg=cg: e.tensor_tensor(
                    out=xo[:, cg * 512:(cg + 1) * 512], in0=ty, in1=xr[:, cg * 512:(cg + 1) * 512], op=ALU.add),
                    reads=[bty, bxr], writes=[bxo])
            if isx:
                dst = out_d[b, ti * 128:(ti + 1) * 128, :]
            else:
                dst = ctx1_d[b, (ti - NTX) * 128:(ti - NTX + 1) * 128, :]
            if l == 0:
                P.dma(dst, xo, bxo, reads=[bxo], writes=[b_x1[b][ti]])
            else:
                P.dma(dst, xo, bxo, reads=[bxo, b_x1[b][ti]], writes=[b_x1[b][ti]])

        for b in range(NB):
            phaseATT0(b)
        P.barrier()
        AR.off = l0_mark

        if stage >= 2:
            la_d = nc.dram_tensor("la", [NB, NT * 128, 1024], F32, kind="Internal").ap()
            sb_d = nc.dram_tensor("sbst", [NB, 32, 128, 1024], BF16, kind="Internal").ap()
            QS = 128.0 ** -0.5
            Wb1 = AR.alloc([8, O_IN], BF16)
            b_Wb1 = Buf("Wb1")
            w1v = win1_d.rearrange("(kc p) n -> p kc n", p=128)
            for c0 in range(0, O_IN, 512):
                n = min(512, O_IN - c0)
                P.dma(Wb1[:, :, c0:c0 + n], w1v[:, :, c0:c0 + n], b_Wb1, writes=[b_Wb1], queue="pool")
            waf = AR.alloc([2, 512], F32)
            b_wa = Buf("wa")
            P.dma(waf[0:16, 0, :], waf_d, b_wa, writes=[b_wa])
            P.dma(waf[0:16, 1, :], wab_d, b_wa, writes=[b_wa])
            barow = AR.alloc([2, 512], F32, parts=1)
            P.dma(barow[0:1, 0, :], baf_d, b_wa, writes=[b_wa])
            P.dma(barow[0:1, 1, :], bab_d, b_wa, writes=[b_wa])
            ones128 = AR.alloc([128], F32, parts=1)
            P.op("pool", lambda e: e.memset(ones128, 1.0), writes=[b_wa])
            xt_r = mkring(AR, "xt1", 2, [D], F32)
            junk = AR.alloc([D], BF16)
            xn_r = mkring(AR, "xn1", 2, [D], BF16)
            hT_r = mkring(AR, "hT1", 2, [8, 128], BF16)
            st_r = mkring(AR, "stat1", 4, [64], F32)
            po_r = mkring(AR, "po1", 2, [3072], BF16)
            rr_r = mkring(AR, "rr", 2, [32], F32)
            rT_r = mkring(AR, "rT", 2, [2, 128], F32)
            la_r = mkring(AR, "las", 2, [1024], F32)
            et_r = mkring(AR, "et", 2, [1024], F32)
            pbank = Ring([banks[0], banks[1]])
            tbank = Ring([banks[2], banks[3]])
            b_pj1 = [[Buf("pj1_%d_%d" % (b, t)) for t in range(NT)] for b in range(NB)]
            b_la = [[Buf("la_%d_%d" % (b, t)) for t in range(NT)] for b in range(NB)]
            cg1 = [(c0, min(512, O_IN - c0)) for c0 in range(0, O_IN, 512)]

            def phaseP1(b):
                for t in range(NT):
                    isx = t < NTX
                    src = out_d[b, t * 128:(t + 1) * 128, :] if isx else ctx1_d[b, (t - NTX) * 128:(t - NTX + 1) * 128, :]
                    hT, bh, _, _ = emit_hT(src, 1, b if isx else NB, b_src_dram=b_x1[b][t])
                    po, bpo = po_r.next()
                    rr, brr = rr_r.next()
                    for gi, (c0, n) in enumerate(cg1):
                        pb, bpb = pbank.next()
                        for kc in range(8):
                            P.op("pe", lambda e, pb=pb, kc=kc, c0=c0, n=n: e.matmul(
                                pb[:, 0:n], lhsT=hT[:, kc, :], rhs=Wb1[:, kc, c0:c0 + n], start=(kc == 0), stop=(kc == 7)),
                                reads=[bh, b_Wb1], writes=[bpb], signal=(kc == 7))
                        if gi == 0:
                            P.op("act", lambda e, pb=pb: e.activation(out=po[:, 0:512], in_=pb, func=AF.Copy, scale=QS),
                                 reads=[bpb], writes=[bpo])
                        elif gi in (1, 3):
                            P.op("dve", lambda e, pb=pb, c0=c0: e.tensor_copy(out=po[:, c0:c0 + 512], in_=pb),
                                 reads=[bpb], writes=[bpo])
                        elif gi == 2:
                            P.op("act", lambda e, pb=pb, c0=c0: e.activation(out=po[:, c0:c0 + 512], in_=pb, func=AF.Copy),
                                 reads=[bpb], writes=[bpo])
                        elif gi in (4, 5):
                            P.op("act", lambda e, pb=pb, c0=c0: e.activation(out=po[:, c0:c0 + 512], in_=pb, func=AF.Silu),
                                 reads=[bpb], writes=[bpo])
                        else:
                            P.op("dve", lambda e, pb=pb: e.tensor_copy(out=rr, in_=pb[:, 0:32]), reads=[bpb], writes=[brr])
                    P.dma(pj_d[b, t * 128:(t + 1) * 128, 0:3072], po, bpo, reads=[bpo], writes=[b_pj1[b][t]])
                    tb, btb = tbank.next()
                    rT, brT = rT_r.next()
                    for dr in range(2):
                        P.op("pe", lambda e, dr=dr, tb=tb: e.transpose(out=tb[0:16, dr * 128:(dr + 1) * 128],
                                                                      in_=rr[:, dr * 16:(dr + 1) * 16],
                                                                      identity=cst[:, C_ID:C_ID + 128]),
                             reads=[brr, b_cst], writes=[btb], signal=(dr == 1))
                    P.op("dve", lambda e, tb=tb: e.tensor_copy(out=rT[0:16], in_=tb[0:16, 0:256].rearrange("p (a t) -> p a t", a=2)),
                         reads=[btb], writes=[brT])
                    las, bls = la_r.next()
                    et, bet = et_r.next()
                    for dr in range(2):
                        pb, bpb = pbank.next()
                        P.op("pe", lambda e, dr=dr, pb=pb: e.matmul(pb, lhsT=rT[0:16, dr, :], rhs=waf[0:16, dr, :],
                                                                   start=True, stop=False),
                             reads=[brT, b_wa], writes=[bpb], signal=False)
                        P.op("pe", lambda e, dr=dr, pb=pb: e.matmul(pb, lhsT=ones128[0:1, :], rhs=barow[0:1, dr, :],
                                                                   start=False, stop=True),
                             reads=[b_wa], writes=[bpb])
                        P.op("act", lambda e, dr=dr, pb=pb: e.activation(out=et[:, dr * 512:(dr + 1) * 512], in_=pb,
                                                                        func=AF.Exp, scale=-1.0),
                             reads=[bpb], writes=[bet])
                    P.op("act", lambda e: e.activation(out=et, in_=et, func=AF.Ln, bias=1.0), reads=[bet], writes=[bet])
                    P.op("pool", lambda e: e.tensor_scalar(out=las, in0=et, scalar1=-1.0 / 16.0, scalar2=None, op0=ALU.mult),
                         reads=[bet], writes=[bls])
                    P.dma(la_d[b, t * 128:(t + 1) * 128, :], las, bls, reads=[bls], writes=[b_la[b][t]])

            for b in range(NB):
                phaseP1(b)
            P.barrier()
            AR.off = l0_mark

            Wob = AR.alloc([8, D], BF16)
            b_Wob = Buf("Wob1")
            P.dma(Wob, wout_d[1].rearrange("(kc p) n -> p kc n", p=128), b_Wob, writes=[b_Wob], queue="pool")
            gbc = AR.alloc([1, D], F32)
            b_gbc = Buf("gbc1")
            onbc = AR.alloc([4, 256], F32)
            b_on = Buf("onbc")
            P.op("dve", lambda e: e.tensor_copy(out=onbc, in_=pv[:, PV_ONORM:PV_ONORM + 256].unsqueeze(1).broadcast_to([128, 4, 256])),
                 reads=[b_pv], writes=[b_on])
            tri = cst[:, C_TRI:C_TRI + 512].rearrange("p (m c) -> p m c", m=4)
            ind = cst[:, C_IND:C_IND + 2]
            kb_r = mkring(AR, "kb", 2, [512], BF16)
            vb_r = mkring(AR, "vb", 2, [1024], BF16)
            lb_r = mkring(AR, "lb", 2, [512], F32)
            eb_r = mkring(AR, "eb", 2, [512], F32)
            kdb_r = mkring(AR, "kdb", 2, [512], BF16)
            decb_r = mkring(AR, "decb", 2, [4, 2], F32)
            Sb = AR.alloc([4, 256], F32)
            b_Sb = Buf("Sb")
            sbo_r = mkring(AR, "sbo", 2, [1024], BF16)
            b_sbd = [[Buf("sbd%d_%d" % (b, c)) for c in range(32)] for b in range(NB)]
            qf_r = mkring(AR, "qf", 2, [512], BF16)
            kf_r = mkring(AR, "kf", 2, [512], BF16)
            vf_r = mkring(AR, "vf", 2, [1024], BF16)
            gf_r = mkring(AR, "gf", 2, [1024], BF16)
            lf_r = mkring(AR, "lf", 2, [1024], F32)
            E_r = mkring(AR, "E", 1, [5, 512], F32)
            pr_r = mkring(AR, "pr", 2, [5, 512], BF16)
            tr_r = mkring(AR, "tr", 2, [4, 4, 128], BF16)
            at_r = mkring(AR, "at", 2, [2, 4, 128], BF16)
            decf_r = mkring(AR, "decf", 2, [4, 2], F32)
            Sf = AR.alloc([4, 256], F32)
            b_Sf = Buf("Sf")
            sfb_r = mkring(AR, "sfb", 4, [1024], BF16)
            sbl_r = mkring(AR, "sbl", 2, [2, 1024], BF16)
            of_r = mkring(AR, "of", 1, [4, 256], F32)
            oq_r = mkring(AR, "oq", 1, [4, 256], F32)
            otok_r = mkring(AR, "otok1", 2, [D], BF16)
            oT_r = mkring(AR, "oT1", 2, [8, 128], BF16)
            xr_r = mkring(AR, "xr1", 2, [D], F32)
            ty_r = mkring(AR, "ty1", 2, [512], F32)
            s2_r = mkring(AR, "s21", 4, [32], F32)
            bring = Ring(banks[0:4])
            pring = Ring([(banks[4], banks[5]), (banks[6], banks[7])])
            T0, YB = banks[2], banks[3]

            def pair_ap(pr):
                i = int(pr[0][1].name[4:])
                return psum_t[:, i:i + 2, :].rearrange("p a c -> p (a c)")

            def emit_gbc1(b):
                for cg in range(2):
                    yb, byb = YB
                    P.op("pe", lambda e, cg=cg, yb=yb: e.matmul(
                        yb, lhsT=cst[0:8, C_SEL + b * 128:C_SEL + (b + 1) * 128],
                        rhs=msbg[0:8, 1, cg * 512:(cg + 1) * 512], start=True, stop=True),
                        reads=[b_cst, b_msb[1]], writes=[byb])
                    P.op("dve", lambda e, cg=cg, yb=yb: e.tensor_copy(out=gbc[:, 0, cg * 512:(cg + 1) * 512], in_=yb),
                         reads=[byb], writes=[b_gbc])

            def tri_mm(m, la_ap, bla):
                bk, bb = bring.next()
                P.op("pe", lambda e: e.matmul(bk, lhsT=tri[:, m, :], rhs=la_ap, start=True, stop=True),
                     reads=[b_cst, bla], writes=[bb])
                return bk, bb

            def dec_mm(la_ap, bla, dec, bdec):
                bk, bb = bring.next()
                for h in range(4):
                    P.op("pe", lambda e, h=h: e.matmul(bk[:, 2 * h:2 * h + 2], lhsT=la_ap[:, h * 128:(h + 1) * 128], rhs=ind,
                                                       start=True, stop=True),
                         reads=[b_cst, bla], writes=[bb], signal=(h == 3))
                P.op("act", lambda e: e.activation(out=dec, in_=bk[:, 0:8].rearrange("p (h c) -> p h c", h=4), func=AF.Exp),
                     reads=[bb], writes=[bdec])

            def phaseB1(b):
                P.op("pool", lambda e: e.memset(Sb, 0.0), writes=[b_Sb])
                for t in list(range(NT - 1, NTX - 1, -1)) + list(range(NTX - 1, -1, -1)):
                    isx = t < NTX
                    rows = slice(t * 128, (t + 1) * 128)
                    kk, bkk = kb_r.next()
                    vv, bvv = vb_r.next()
                    lb, blb = lb_r.next()
                    P.dma(kk, pj_d[b, rows, 512:1024], bkk, reads=[b_pj1[b][t]], writes=[bkk])
                    P.dma(vv, pj_d[b, rows, 1024:2048], bvv, reads=[b_pj1[b][t]], writes=[bvv])
                    P.dma(lb, la_d[b, rows, 512:1024], blb, reads=[b_la[b][t]], writes=[blb])
                    bk, bb = tri_mm(3, lb, blb)
                    eb, beb = eb_r.next()
                    P.op("act", lambda e: e.activation(out=eb, in_=bk, func=AF.Exp), reads=[bb], writes=[beb])
                    dec, bdec = decb_r.next()
                    dec_mm(lb, blb, dec, bdec)
                    kd, bkd = kdb_r.next()
                    P.op("pool", lambda e: e.tensor_tensor(out=kd, in0=kk, in1=eb, op=ALU.mult), reads=[bkk, beb], writes=[bkd])
                    for ci in (1, 0):
                        cr = slice(ci * 64, (ci + 1) * 64)
                        pr = pring.next()
                        pap = pair_ap(pr).rearrange("p (h e) -> p h e", h=4)
                        for h in range(4):
                            P.op("pe", lambda e, h=h, cr=cr, pap=pap: e.matmul(
                                pap[:, h, :], lhsT=kd[cr, h * 128:(h + 1) * 128], rhs=vv[cr, h * 256:(h + 1) * 256],
                                start=True, stop=True), reads=[bkd, bvv], writes=[pr[0][1], pr[1][1]], signal=(h == 3))
                        if isx:
                            so, bso = sbo_r.next()
                            P.op("act", lambda e, so=so: e.activation(out=so, in_=Sb.rearrange("p h e -> p (h e)"), func=AF.Copy),
                                 reads=[b_Sb], writes=[bso])
                            cidx = t * 2 + ci
                            P.dma(sb_d[b, cidx], so, bso, reads=[bso], writes=[b_sbd[b][cidx]])
                        for h in range(4):
                            P.op("dve", lambda e, h=h, ci=ci, pap=pap: e.scalar_tensor_tensor(
                                out=Sb[:, h, :], in0=Sb[:, h, :], scalar=dec[:, h, ci:ci + 1], in1=pap[:, h, :],
                                op0=ALU.mult, op1=ALU.add), reads=[b_Sb, bdec, pr[0][1], pr[1][1]], writes=[b_Sb])

            def phaseF1(b):
                emit_gbc1(b)
                P.op("pool", lambda e: e.memset(Sf, 0.0), writes=[b_Sf])
                for t in [NTX, NTX + 1] + list(range(NTX)):
                    isx = t < NTX
                    rows = slice(t * 128, (t + 1) * 128)
                    rd = [b_pj1[b][t]]
                    kk, bkk = kf_r.next()
                    vv, bvv = vf_r.next()
                    lf, blf = lf_r.next()
                    P.dma(kk, pj_d[b, rows, 512:1024], bkk, reads=rd, writes=[bkk])
                    P.dma(vv, pj_d[b, rows, 1024:2048], bvv, reads=rd, writes=[bvv])
                    P.dma(lf, la_d[b, rows, :], blf, reads=[b_la[b][t]], writes=[blf])
                    E, bE = E_r.next()
                    pr5, bpr = pr_r.next()
                    dec, bdec = decf_r.next()
                    bk, bb = tri_mm(1, lf[:, 0:512], blf)
                    P.op("act", lambda e, bk=bk: e.activation(out=E[:, 2, :], in_=bk, func=AF.Exp), reads=[bb], writes=[bE])
                    dec_mm(lf[:, 0:512], blf, dec, bdec)
                    P.op("pool", lambda e: e.tensor_tensor(out=pr5[:, 2, :], in0=kk, in1=E[:, 2, :], op=ALU.mult),
                         reads=[bkk, bE], writes=[bpr])
                    if isx:
                        qq, bqq = qf_r.next()
                        gg, bgg = gf_r.next()
                        P.dma(qq, pj_d[b, rows, 0:512], bqq, reads=rd, writes=[bqq])
                        P.dma(gg, pj_d[b, rows, 2048:3072], bgg, reads=rd, writes=[bgg])
                        sbl, bsbl = sbl_r.next()
                        P.dma(sbl, sb_d[b, 2 * t:2 * t + 2].rearrange("c p n -> p c n"), bsbl,
                              reads=[b_sbd[b][2 * t], b_sbd[b][2 * t + 1]], writes=[bsbl])
                        bk, bb = tri_mm(0, lf[:, 0:512], blf)
                        P.op("act", lambda e, bk=bk: e.activation(out=E[:, 0, :], in_=bk, func=AF.Exp), reads=[bb], writes=[bE])
                        P.op("act", lambda e, bk=bk: e.activation(out=E[:, 1, :], in_=bk, func=AF.Exp, scale=-1.0),
                             reads=[bb], writes=[bE])
                        bk, bb = tri_mm(2, lf[:, 512:1024], blf)
                        P.op("act", lambda e, bk=bk: e.activation(out=E[:, 3, :], in_=bk, func=AF.Exp), reads=[bb], writes=[bE])
                        P.op("act", lambda e, bk=bk: e.activation(out=E[:, 4, :], in_=bk, func=AF.Exp, scale=-1.0),
                             reads=[bb], writes=[bE])
                        for (slot, src, bsrc, ei, eng) in ((0, qq, bqq, 0, "dve"), (1, kk, bkk, 1, "pool"),
                                                           (3, qq, bqq, 3, "dve"), (4, kk, bkk, 4, "pool")):
                            P.op(eng, lambda e, slot=slot, src=src, ei=ei: e.tensor_tensor(
                                out=pr5[:, slot, :], in0=src, in1=E[:, ei, :], op=ALU.mult),
                                reads=[bsrc, bE], writes=[bpr])
                        trt, btr = tr_r.next()
                        for wi, slot in enumerate((0, 1, 3, 4)):
                            tb, btb = bring.next()
                            tbv = tb.bitcast(BF16)[:, 0:512].rearrange("p (h t) -> p h t", h=4)
                            for h in range(4):
                                P.op("pe", lambda e, h=h, slot=slot, tbv=tbv: e.transpose(
                                    out=tbv[:, h, :], in_=pr5[:, slot, h * 128:(h + 1) * 128], identity=identb),
                                    reads=[bpr, b_cb], writes=[btb], signal=(h == 3))
                            if wi % 2 == 0:
                                P.op("act", lambda e, wi=wi, tbv=tbv: e.activation(out=trt[:, wi], in_=tbv, func=AF.Copy),
                                     reads=[btb], writes=[btr])
                            else:
                                P.op("dve", lambda e, wi=wi, tbv=tbv: e.tensor_copy(out=trt[:, wi], in_=tbv),
                                     reads=[btb], writes=[btr])
                        att, bat = at_r.next()
                        for di in range(2):
                            ab, bab = bring.next()
                            abv = ab.rearrange("p (h c) -> p h c", h=4)
                            for h in range(4):
                                P.op("pe", lambda e, h=h, di=di, abv=abv: e.matmul(
                                    abv[:, h, :], lhsT=trt[:, 2 * di + 1, h, :], rhs=trt[:, 2 * di, h, :], start=True, stop=True),
                                    reads=[btr], writes=[bab], signal=(h == 3))
                            m = 0 if di == 0 else 2
                            P.op("dve", lambda e, di=di, abv=abv, m=m: e.tensor_tensor(
                                out=att[:, di], in0=abv, in1=tri[:, m, :].unsqueeze(1).broadcast_to([128, 4, 128]), op=ALU.mult),
                                reads=[bab, b_cst], writes=[bat])
                        P.op("pool", lambda e: e.tensor_tensor(out=att[:, 0], in0=att[:, 0], in1=att[:, 1], op=ALU.add),
                             reads=[bat], writes=[bat])
                    if isx:
                        opr = pring.next()
                        oap = pair_ap(opr).rearrange("p (h e) -> p h e", h=4)
                    sfbs = []
                    for ci in (0, 1):
                        cr = slice(ci * 64, (ci + 1) * 64)
                        if isx:
                            sfb, bsfb = sfb_r.next()
                            P.op("act", lambda e, sfb=sfb: e.activation(out=sfb, in_=Sf.rearrange("p h e -> p (h e)"), func=AF.Copy),
                                 reads=[b_Sf], writes=[bsfb])
                            sfbs.append((sfb, bsfb))
                        pr = pring.next()
                        pap = pair_ap(pr).rearrange("p (h e) -> p h e", h=4)
                        for h in range(4):
                            P.op("pe", lambda e, h=h, cr=cr, pap=pap: e.matmul(
                                pap[:, h, :], lhsT=pr5[cr, 2, h * 128:(h + 1) * 128], rhs=vv[cr, h * 256:(h + 1) * 256],
                                start=True, stop=True), reads=[bpr, bvv], writes=[pr[0][1], pr[1][1]], signal=(h == 3))
                        for h in range(4):
                            P.op("dve", lambda e, h=h, ci=ci, pap=pap: e.scalar_tensor_tensor(
                                out=Sf[:, h, :], in0=Sf[:, h, :], scalar=dec[:, h, ci:ci + 1], in1=pap[:, h, :],
                                op0=ALU.mult, op1=ALU.add), reads=[b_Sf, bdec, pr[0][1], pr[1][1]], writes=[b_Sf])
                    if not isx:
                        continue
                    for h in range(4):
                        P.op("pe", lambda e, h=h: e.matmul(oap[:, h, :], lhsT=att[:, 0, h, :], rhs=vv[:, h * 256:(h + 1) * 256],
                                                           start=True, stop=False),
                             reads=[bat, bvv], writes=[opr[0][1], opr[1][1]], signal=False)
                        for ci in (0, 1):
                            cs = slice(ci * 64, (ci + 1) * 64)
                            sfb, bsfb = sfbs[ci]
                            P.op("pe", lambda e, h=h, cs=cs, sfb=sfb: e.matmul(
                                oap[cs, h, :], lhsT=trt[:, 0, h, cs], rhs=sfb[:, h * 256:(h + 1) * 256], start=False, stop=False),
                                reads=[btr, bsfb], writes=[opr[0][1], opr[1][1]], signal=False)
                            P.op("pe", lambda e, h=h, cs=cs, ci=ci: e.matmul(
                                oap[cs, h, :], lhsT=trt[:, 2, h, cs], rhs=sbl[:, ci, h * 256:(h + 1) * 256], start=False,
                                stop=(ci == 1)), reads=[btr, bsbl], writes=[opr[0][1], opr[1][1]], signal=(ci == 1))
                    of, bof = of_r.next()
                    oq, boq = oq_r.next()
                    s2, bs2 = s2_r.next()
                    otok, bot = otok_r.next()
                    P.op("act", lambda e: e.activation(out=of, in_=oap, func=AF.Copy), reads=[opr[0][1], opr[1][1]], writes=[bof])
                    P.op("pool", lambda e: e.tensor_tensor(out=oq, in0=of, in1=of, op=ALU.mult), reads=[bof], writes=[boq])
                    P.op("dve", lambda e: e.tensor_reduce(out=s2[:, 0:4], in_=oq, axis=AXX, op=ALU.add), reads=[boq], writes=[bs2])
                    P.op("dve", lambda e: e.tensor_scalar(out=s2[:, 0:4], in0=s2[:, 0:4], scalar1=1.0 / 256, scalar2=EPS,
                                                          op0=ALU.mult, op1=ALU.add), reads=[bs2], writes=[bs2])
                    P.op("pool", lambda e: e.tensor_tensor(out=s2[:, 4:8], in0=s2[:, 0:4], in1=nhalf[:, 0:4], op=ALU.pow),
                         reads=[bs2, b_cb], writes=[bs2])
                    P.op("dve", lambda e: e.tensor_tensor(out=oq, in0=of, in1=s2[:, 4:8].unsqueeze(2).broadcast_to([128, 4, 256]),
                                                          op=ALU.mult), reads=[bof, bs2], writes=[boq])
                    P.op("pool", lambda e: e.tensor_tensor(out=of, in0=oq, in1=onbc, op=ALU.mult), reads=[boq, b_on], writes=[bof])
                    P.op("dve", lambda e: e.tensor_tensor(out=otok, in0=of.rearrange("p h e -> p (h e)"), in1=gg, op=ALU.mult),
                         reads=[bof, bgg], writes=[bot])
                    emit_outproj(b, t, otok, bot, 0, True, 1)

            for b in range(NB):
                phaseB1(b)
                phaseF1(b)
            P.barrier()

        for b in range(NB):
            for t in range(NT):
                if b_x1[b][t].w is not None:
                    P._dep("sp", b_x1[b][t].w)
        P.finish()
        print("program: nsem=%d" % P.nsem, {e: len(P.q[e]) for e in P.ENGS})
    return nc


def host_inputs(inputs, NB, cores):
    c = np.asarray(inputs["c"], np.float32)
    cctx = np.asarray(inputs["c_ctx"], np.float32)
    x = np.asarray(inputs["x"], np.float32)
    ctx = np.asarray(inputs["ctx"], np.float32)
    consts, rope = make_consts()
    pvec = np.zeros((PV_W,), np.float32)

    def put(off, a):
        a = np.asarray(a, np.float32).reshape(-1)
        pvec[off:off + a.size] = a

    put(PV_AQ, inputs["a_q_norm"]); put(PV_AK, inputs["a_k_norm"])
    put(PV_BQ, inputs["b_q_norm"]); put(PV_BK, inputs["b_k_norm"])
    put(PV_SINK, inputs["a_sink"]); put(PV_SUBLN, inputs["b_subln"])
    put(PV_LQ1, inputs["b_lambda_q1"]); put(PV_LK1, inputs["b_lambda_k1"])
    put(PV_LQ2, inputs["b_lambda_q2"]); put(PV_LK2, inputs["b_lambda_k2"])
    put(PV_ONORM, inputs["gla_out_norm"])
    ng = np.asarray(inputs["norm_g"], np.float32)
    normgT = np.ascontiguousarray(ng.reshape(2, 8, 128).transpose(0, 2, 1))
    shared = {
        "adaln_w": np.ascontiguousarray(inputs["adaln_w"], np.float32),
        "adaln_b": np.ascontiguousarray(inputs["adaln_b"], np.float32),
        "normgT": normgT,
        "w_out": np.ascontiguousarray(inputs["w_out"], np.float32),
        "ab_w_in": np.ascontiguousarray(np.asarray(inputs["ab_w_in"], np.float32)[0]),
        "gla_w_in": np.ascontiguousarray(np.asarray(inputs["gla_w_in"], np.float32)[0]),
        "gla_wa_f": np.ascontiguousarray(np.asarray(inputs["gla_wa_f"], np.float32)[0]),
        "gla_wa_b": np.ascontiguousarray(np.asarray(inputs["gla_wa_b"], np.float32)[0]),
        "pvec": pvec,
        "gla_ba_f": np.ascontiguousarray(np.asarray(inputs["gla_ba_f"], np.float32).reshape(1, 512)),
        "gla_ba_b": np.ascontiguousarray(np.asarray(inputs["gla_ba_b"], np.float32).reshape(1, 512)),
        "consts": consts,
        "rope": rope,
    }
    maps = []
    for ci in cores:
        rows = np.zeros((8, D), np.float32)
        rows[0:NB] = c[ci * NB:(ci + 1) * NB]
        rows[NB] = cctx
        cT = np.ascontiguousarray(rows.reshape(8, 8, 128).transpose(2, 1, 0))
        m = dict(shared)
        m["x"] = np.ascontiguousarray(x[ci * NB:(ci + 1) * NB])
        m["ctx"] = np.ascontiguousarray(ctx[ci * NB:(ci + 1) * NB])
        m["cT"] = cT
        maps.append(m)
    return maps


_NC_CACHE = {}


def kernel(**inputs):
    NB = 4
    if NB not in _NC_CACHE:
        _NC_CACHE[NB] = build(NB, stage=2)
    nc = _NC_CACHE[NB]
    maps = host_inputs(inputs, NB, list(range(8)))
    res = run_bass_kernel_spmd(nc, maps, core_ids=list(range(8)))
    out = np.concatenate([np.asarray(r["out"]) for r in res.results], axis=0)
    return out.astype(np.float32)
```

```python
import math
from contextlib import ExitStack

import numpy as np
import concourse.bass as bass
import concourse.mybir as mybir
from concourse.bass_utils import run_bass_kernel_spmd
from concourse.alu_op_type import AluOpType as ALU

F32 = mybir.dt.float32
BF16 = mybir.dt.bfloat16
U8 = mybir.dt.uint8
AF = mybir.ActivationFunctionType
AXX = mybir.AxisListType.X

D = 1024
T = 2048
LC = 256
NTX = 16
NT = 18
EPS = 1e-6
E_IN = 3328
O_IN = 3104
ARENA_BYTES = 200 * 1024


def _dsize(dt):
    return 4 if dt == F32 else (2 if dt == BF16 else 1)


class Buf:
    def __init__(self, name):
        self.name = name
        self.w = None
        self.r = {}
        self.dsem = None
        self.dcnt = 0


class _Rec:
    def __init__(self):
        self.call = None

    def __getattr__(self, name):
        def f(*a, **k):
            self.call = (name, a, k)
            return self
        return f


class Prog:
    ENGS = ("pe", "act", "dve", "pool", "sp")

    def __init__(self, nc, stack):
        self.nc = nc
        self.stack = stack
        self.q = {e: [] for e in self.ENGS}
        self.sem = {e: stack.enter_context(nc.semaphore("s_" + e)) for e in ("pe", "act", "dve", "pool")}
        self.cnt = {e: 0 for e in ("pe", "act", "dve", "pool")}
        self.waited = {e: {} for e in self.ENGS}
        self.semof = {}
        self.dbufs = []
        self.nsem = 4

    def _dep(self, eng, dep, same_ok=True):
        sem, val = dep
        key = id(sem)
        own = self.sem.get(eng)
        if own is not None and sem is own:
            if eng == "pe":
                return
            if val > self.cnt[eng]:
                return
        if self.waited[eng].get(key, 0) >= val:
            return
        self.waited[eng][key] = val
        self.q[eng].append(("wait", sem, val))

    def op(self, eng, fn, reads=(), writes=(), signal=True):
        for b in reads:
            if b.w is not None:
                self._dep(eng, b.w)
        for b in writes:
            if b.w is not None:
                self._dep(eng, b.w)
            for k, d in b.r.items():
                if d[0] is self.sem.get(eng):
                    continue
                self._dep(eng, d)
        mysem = self.sem[eng]
        val = self.cnt[eng] + 1
        rec = _Rec()
        fn(rec)
        assert rec.call is not None
        self.q[eng].append(("inst", rec.call, mysem if signal else None))
        if signal:
            self.cnt[eng] = val
        for b in reads:
            b.r[id(mysem)] = (mysem, val)
        for b in writes:
            b.w = (mysem, val)
            b.r = {}

    def dma(self, out_ap, in_ap, sembuf, reads=(), writes=(), queue="sp"):
        eng = queue
        for b in reads:
            if b.w is not None:
                self._dep(eng, b.w)
        for b in writes:
            if b.w is not None:
                self._dep(eng, b.w)
            for k, d in b.r.items():
                self._dep(eng, d)
        if sembuf.dsem is None:
            sembuf.dsem = self.stack.enter_context(self.nc.semaphore("d_" + sembuf.name))
            self.nsem += 1
            self.dbufs.append(sembuf)
        sembuf.dcnt += 16
        self.q[eng].append(("dma", out_ap, in_ap, sembuf.dsem))
        d = (sembuf.dsem, sembuf.dcnt)
        for b in reads:
            b.r[id(sembuf.dsem)] = d
        for b in writes:
            b.w = d
            b.r = {}

    def barrier(self):
        deps = [(self.sem[e], self.cnt[e]) for e in ("pe", "act", "dve", "pool") if self.cnt[e] > 0]
        deps += [(b.dsem, b.dcnt) for b in self.dbufs]
        for e in self.ENGS:
            for d in deps:
                self._dep(e, d)

    def finish(self):
        nc = self.nc
        engobj = {"pe": "tensor", "act": "scalar", "dve": "vector", "pool": "gpsimd", "sp": "sync"}
        with nc.Block() as block:
            for e in self.ENGS:
                items = self.q[e]

                def body(eng, items=items):
                    for it in items:
                        if it[0] == "wait":
                            eng.wait_ge(it[1], it[2])
                        elif it[0] == "inst":
                            name, a, k = it[1]
                            r = getattr(eng, name)(*a, **k)
                            if it[2] is not None:
                                r.then_inc(it[2], 1)
                        else:
                            eng.dma_start(out=it[1], in_=it[2]).then_inc(it[3], 16)

                getattr(block, engobj[e])(body)


class Arena:
    def __init__(self, ap, size):
        self.ap = ap
        self.size = size
        self.off = 0

    def alloc(self, shape, dt, parts=128):
        n = int(np.prod(shape)) * _dsize(dt)
        off = (self.off + 63) // 64 * 64
        assert off + n <= self.size, ("arena overflow", off, n, self.size)
        a = self.ap[0:parts, off:off + n].bitcast(dt)
        if len(shape) > 1:
            names = ["d%d" % i for i in range(len(shape))]
            pat = "p (" + " ".join(names) + ") -> p " + " ".join(names)
            a = a.rearrange(pat, **{names[i]: int(shape[i]) for i in range(len(shape))})
        self.off = off + n
        return a


class Ring:
    def __init__(self, items):
        self.items = items
        self.i = 0

    def next(self):
        it = self.items[self.i % len(self.items)]
        self.i += 1
        return it


def mkring(ar, name, n, shape, dt, parts=128):
    return Ring([(ar.alloc(shape, dt, parts), Buf("%s%d" % (name, i))) for i in range(n)])


C_ID = 0
C_ML = 128
C_MR = 256
C_TRI = 384
C_IND = 896
C_SEL = 898
C_W = C_SEL + 8 * 128


def make_consts():
    c = np.zeros((128, C_W), np.float32)
    a = np.arange(128)
    c[:, C_ID:C_ID + 128] = np.eye(128, dtype=np.float32)
    c[:, C_ML:C_ML + 128] = (a[None, :] <= a[:, None]).astype(np.float32)
    c[:, C_MR:C_MR + 128] = (a[:, None] <= a[None, :]).astype(np.float32)
    same = (a[:, None] // 64) == (a[None, :] // 64)
    s_ = a[:, None]
    c_ = a[None, :]
    c[:, C_TRI + 0:C_TRI + 128] = (same & (s_ <= c_)).astype(np.float32)
    c[:, C_TRI + 128:C_TRI + 256] = (same & (s_ > c_)).astype(np.float32)
    c[:, C_TRI + 256:C_TRI + 384] = (same & (s_ >= c_)).astype(np.float32)
    c[:, C_TRI + 384:C_TRI + 512] = (same & (s_ < c_)).astype(np.float32)
    c[:, C_IND + 0] = (a < 64)
    c[:, C_IND + 1] = (a >= 64)
    inv = (10000.0 ** (-np.arange(16, dtype=np.float32) / 16.0)).astype(np.float32)
    tok = np.arange(T)
    rows = (tok // 64).astype(np.float32)
    cols = (tok % 64).astype(np.float32)
    ar = rows[:, None] * inv[None, :]
    ac = cols[:, None] * inv[None, :]
    cr, sr, cc, sc = np.cos(ar), np.sin(ar), np.cos(ac), np.sin(ac)
    Ct = np.concatenate([cr, cr, cc, cc], axis=1).astype(np.float32)
    St = np.concatenate([-sr, sr, -sc, sc], axis=1).astype(np.float32)
    rope = np.zeros((128, 2048), np.float32)
    rope[:, 0:1024] = Ct.reshape(16, 128, 64).transpose(1, 0, 2).reshape(128, 1024)
    rope[:, 1024:2048] = St.reshape(16, 128, 64).transpose(1, 0, 2).reshape(128, 1024)
    for r in range(8):
        c[r, C_SEL + r * 128:C_SEL + (r + 1) * 128] = 1.0
    return c, rope


PV_AQ, PV_AK, PV_BQ, PV_BK = 0, 64, 128, 192
PV_SINK = 256
PV_SUBLN = 264
PV_LQ1, PV_LK1, PV_LQ2, PV_LK2 = 392, 456, 520, 584
PV_ONORM = 648
PV_W = 904


def build(NB, stage=2):
    nc = bass.Bass("TRN2", target_bir_lowering=False)

    def din(name, shape, dt=F32):
        return nc.dram_tensor(name, list(shape), dt, kind="ExternalInput").ap()

    x_d = din("x", [NB, NT * 128, D])
    cT_d = din("cT", [128, 8, 8])
    adw_d = din("adaln_w", [2, D, 3 * D])
    adb_d = din("adaln_b", [2, 3 * D])
    ngT_d = din("normgT", [2, 128, 8])
    wout_d = din("w_out", [2, D, D])
    win0_d = din("ab_w_in", [D, E_IN])
    win1_d = din("gla_w_in", [D, O_IN])
    waf_d = din("gla_wa_f", [16, 512])
    wab_d = din("gla_wa_b", [16, 512])
    pv_d = din("pvec", [PV_W])
    baf_d = din("gla_ba_f", [1, 512])
    bab_d = din("gla_ba_b", [1, 512])
    cst_d = din("consts", [128, C_W])
    rope_d = din("rope", [128, 2048])
    out_d = nc.dram_tensor("out", [NB, NT * 128, D], F32, kind="ExternalOutput").ap()
    pj_d = nc.dram_tensor("pj", [NB, NT * 128, E_IN], BF16, kind="Internal").ap()

    with ExitStack() as st:
        P = Prog(nc, st)
        arena_t = st.enter_context(nc.sbuf_tensor("arena", [128, ARENA_BYTES], U8))
        psum_t = st.enter_context(nc.psum_tensor("psum", [128, 8, 512], F32))
        AR = Arena(arena_t, ARENA_BYTES)
        banks = [(psum_t[:, i, :], Buf("bank%d" % i)) for i in range(8)]

        cst = AR.alloc([C_W], F32)
        b_cst = Buf("cst")
        pv = AR.alloc([PV_W], F32)
        b_pv = Buf("pv")
        identb = AR.alloc([128], BF16)
        maskb = AR.alloc([2, 4, 128], BF16)
        b_cb = Buf("constb")
        nhalf = AR.alloc([32], F32)
        ones_f = AR.alloc([8], F32)
        scT = AR.alloc([8, 8], F32)
        b_scT = Buf("scT")
        msbg = AR.alloc([2, D], F32)
        b_msb = [Buf("msb0"), Buf("msb1")]
        scl = AR.alloc([2, 8, 8], F32)
        shf = AR.alloc([2, 8, 8], F32)
        b_mod = [Buf("mod0"), Buf("mod1")]
        ngT = AR.alloc([2, 8], F32)
        b_ngT = Buf("ngT")
        gain_all = AR.alloc([26, 64], F32)
        esink = AR.alloc([8], F32)
        neglam = AR.alloc([1], F32)
        subln8 = AR.alloc([128], F32)
        b_par = Buf("par")
        persist_mark = AR.off

        def V(fn):
            return fn

        P.dma(cst, cst_d, b_cst, writes=[b_cst])
        P.dma(pv, pv_d.partition_broadcast(128), b_pv, writes=[b_pv])
        P.dma(ngT, ngT_d.rearrange("l p k -> p l k"), b_ngT, writes=[b_ngT])
        P.dma(scT, cT_d, b_scT, writes=[b_scT])

        P.op("pool", lambda e: e.memset(nhalf, -0.5), writes=[b_cb])
        P.op("pool", lambda e: e.memset(ones_f, 1.0), writes=[b_cb])
        P.op("dve", lambda e: e.tensor_copy(out=identb, in_=cst[:, C_ID:C_ID + 128]), reads=[b_cst], writes=[b_cb])
        P.op("dve", lambda e: e.tensor_copy(
            out=maskb[:, 0], in_=cst[:, C_ML:C_ML + 128].unsqueeze(1).broadcast_to([128, 4, 128])),
            reads=[b_cst], writes=[b_cb])
        P.op("dve", lambda e: e.tensor_copy(
            out=maskb[:, 1], in_=cst[:, C_MR:C_MR + 128].unsqueeze(1).broadcast_to([128, 4, 128])),
            reads=[b_cst], writes=[b_cb])
        for (h0, n, off) in ((0, 8, PV_AQ), (8, 2, PV_AK), (10, 8, PV_BQ), (18, 8, PV_BK)):
            P.op("dve", lambda e, h0=h0, n=n, off=off: e.tensor_copy(
                out=gain_all[:, h0:h0 + n, :], in_=pv[:, off:off + 64].unsqueeze(1).broadcast_to([128, n, 64])),
                reads=[b_pv], writes=[b_par])
        P.op("act", lambda e: e.activation(out=esink, in_=pv[:, PV_SINK:PV_SINK + 8], func=AF.Exp),
             reads=[b_pv], writes=[b_par])
        ltmp = AR.alloc([64], F32)
        lred = AR.alloc([2], F32)
        lexp = AR.alloc([2], F32)
        b_l = Buf("ltmp")
        for i, (oa, ob) in enumerate(((PV_LQ1, PV_LK1), (PV_LQ2, PV_LK2))):
            P.op("dve", lambda e, oa=oa, ob=ob: e.tensor_tensor(
                out=ltmp, in0=pv[:, oa:oa + 64], in1=pv[:, ob:ob + 64], op=ALU.mult), reads=[b_pv], writes=[b_l])
            P.op("dve", lambda e, i=i: e.tensor_reduce(out=lred[:, i:i + 1], in_=ltmp, axis=AXX, op=ALU.add),
                 reads=[b_l], writes=[b_l])
        P.op("act", lambda e: e.activation(out=lexp, in_=lred, func=AF.Exp), reads=[b_l], writes=[b_l])
        lam_init0 = 0.8 - 0.6 * math.exp(-0.3 * 0)
        P.op("dve", lambda e: e.tensor_tensor(out=neglam, in0=lexp[:, 1:2], in1=lexp[:, 0:1], op=ALU.subtract),
             reads=[b_l], writes=[b_par])
        P.op("dve", lambda e: e.tensor_scalar(out=neglam, in0=neglam, scalar1=-lam_init0, scalar2=None, op0=ALU.add),
             reads=[b_par], writes=[b_par])
        P.op("dve", lambda e: e.tensor_scalar(out=subln8, in0=pv[:, PV_SUBLN:PV_SUBLN + 128],
                                              scalar1=1.0 - lam_init0, scalar2=None, op0=ALU.mult),
             reads=[b_pv], writes=[b_par])
        P.op("act", lambda e: e.activation(out=scT, in_=scT, func=AF.Silu), reads=[b_scT], writes=[b_scT])

        setup_mark = AR.off
        wst = mkring(AR, "wst", 2, [3 * D], F32)
        msb = AR.alloc([3 * D], F32)
        b_msbt = Buf("msbt")
        brow = AR.alloc([3 * D], F32, parts=1)
        b_brow = Buf("brow")
        for l in range(2):
            P.dma(brow, adb_d[l:l + 1, :], b_brow, writes=[b_brow])
            for kc in range(8):
                wt, bw = wst.next()
                P.dma(wt, adw_d[l, kc * 128:(kc + 1) * 128, :], bw, writes=[bw])
                for g in range(6):
                    bk, bb = banks[g]
                    P.op("pe", lambda e, bk=bk, kc=kc, wt=wt, g=g: e.matmul(
                        bk[0:8, :], lhsT=scT[:, kc, :], rhs=wt[:, g * 512:(g + 1) * 512],
                        start=(kc == 0), stop=False), reads=[b_scT, bw], writes=[bb], signal=(g == 5))
            for g in range(6):
                bk, bb = banks[g]
                P.op("pe", lambda e, bk=bk, g=g: e.matmul(
                    bk[0:8, :], lhsT=ones_f[0:1, :], rhs=brow[0:1, g * 512:(g + 1) * 512],
                    start=False, stop=True), reads=[b_cb, b_brow], writes=[bb])
                P.op("dve" if g % 2 else "act",
                     (lambda e, bk=bk, g=g, l=l: e.tensor_copy(out=msb[0:8, g * 512:(g + 1) * 512], in_=bk[0:8, :]))
                     if g % 2 else
                     (lambda e, bk=bk, g=g, l=l: e.activation(out=msb[0:8, g * 512:(g + 1) * 512], in_=bk[0:8, :],
                                                              func=AF.Copy)),
                     reads=[bb], writes=[b_msbt])
            P.op("dve", lambda e, l=l: e.tensor_copy(out=msbg[0:8, l, :], in_=msb[0:8, 2048:3072]),
                 reads=[b_msbt], writes=[b_msb[l]])
            bk, bb = banks[6]
            tpv = bk[:, 0:128].rearrange("p (c r) -> p c r", c=16)
            for c in range(16):
                P.op("pe", lambda e, c=c, l=l, tpv=tpv: e.transpose(
                    out=tpv[:, c, :], in_=msb[0:8, c * 128:(c + 1) * 128], identity=cst[0:8, C_ID:C_ID + 8]),
                    reads=[b_msbt, b_cst], writes=[bb], signal=(c == 15))
            P.op("dve", lambda e, l=l, tpv=tpv: e.tensor_copy(out=shf[:, l], in_=tpv[:, 0:8, :]),
                 reads=[bb], writes=[b_mod[l]])
            P.op("dve", lambda e, l=l, tpv=tpv: e.tensor_scalar(
                out=scl[:, l], in0=tpv[:, 8:16, :], scalar1=1.0, scalar2=None, op0=ALU.add),
                reads=[bb], writes=[b_mod[l]])
            P.op("dve", lambda e, l=l: e.tensor_tensor(
                out=scl[:, l], in0=scl[:, l], in1=ngT[:, l, :].unsqueeze(2).broadcast_to([128, 8, 8]), op=ALU.mult),
                reads=[b_mod[l], b_ngT], writes=[b_mod[l]])
        P.barrier()
        AR.off = setup_mark

        l0_mark = AR.off
        Wb = AR.alloc([8, E_IN], BF16)
        b_Wb = Buf("Wb")
        blocks = ((0, 512, 0), (512, 128, 512), (1280, 512, 640), (1792, 512, 1152),
                  (2304, 512, 1664), (640, 128, 2176), (768, 512, 2304), (2816, 512, 2816))
        w0v = win0_d.rearrange("(kc p) n -> p kc n", p=128)
        for (src, n, dst) in blocks:
            P.dma(Wb[:, :, dst:dst + n], w0v[:, :, src:src + n], b_Wb, writes=[b_Wb], queue="pool")
        colgroups = ((0, 512), (512, 512), (1024, 512), (1536, 128), (1664, 512), (2176, 128),
                     (2304, 512), (2816, 512))

        xt_r = mkring(AR, "xt", 2, [D], F32)
        junk = AR.alloc([D], BF16)
        xn_r = mkring(AR, "xn", 2, [D], BF16)
        hT_r = mkring(AR, "hT", 2, [8, 128], BF16)
        qk_r = mkring(AR, "qk", 2, [26, 64], F32)
        tA_r = mkring(AR, "tA", 2, [26, 64], F32)
        tB_r = mkring(AR, "tB", 2, [26, 64], F32)
        tC_r = mkring(AR, "tC", 2, [26, 64], F32)
        po_r = mkring(AR, "po", 2, [E_IN], BF16)
        st_r = mkring(AR, "stat", 4, [64], F32)
        pbank = Ring([banks[0], banks[1]])
        tbank = Ring([banks[2], banks[3]])
        b_pj = [[Buf("pj%d_%d" % (b, t)) for t in range(NT)] for b in range(NB)]
        ropet = AR.alloc([2, 16, 64], F32)
        b_rope = Buf("rope")
        P.dma(ropet, rope_d.rearrange("p (a t d) -> p a t d", a=2, t=16), b_rope, writes=[b_rope])
        ropeC = ropet[:, 0]
        ropeS = ropet[:, 1]

        def emit_hT(src_ap, l, r, b_src_dram=None):
            xt, bx = xt_r.next()
            P.dma(xt, src_ap, bx, reads=([b_src_dram] if b_src_dram else []), writes=[bx])
            stt, bs = st_r.next()
            P.op("act", lambda e: e.activation(out=junk, in_=xt, func=AF.Square, accum_out=stt[:, 0:1]),
                 reads=[bx], writes=[bs])
            P.op("dve", lambda e: e.tensor_scalar(out=stt[:, 1:2], in0=stt[:, 0:1], scalar1=1.0 / D, scalar2=EPS,
                                                  op0=ALU.mult, op1=ALU.add), reads=[bs], writes=[bs])
            P.op("pool", lambda e: e.tensor_tensor(out=stt[:, 2:3], in0=stt[:, 1:2], in1=nhalf[:, 0:1], op=ALU.pow),
                 reads=[bs, b_cb], writes=[bs])
            xn, bn = xn_r.next()
            P.op("act", lambda e: e.activation(out=xn, in_=xt, func=AF.Copy, scale=stt[:, 2:3]),
                 reads=[bx, bs], writes=[bn])
            tb, btb = tbank.next()
            tpb = tb.bitcast(BF16).rearrange("p (k t) -> p k t", k=8)
            for kc in range(8):
                P.op("pe", lambda e, kc=kc: e.transpose(out=tpb[:, kc, :], in_=xn[:, kc * 128:(kc + 1) * 128],
                                                        identity=identb),
                     reads=[bn, b_cb], writes=[btb], signal=(kc == 7))
            hT, bh = hT_r.next()
            for kc in range(8):
                if kc % 2 == 0:
                    P.op("dve", lambda e, kc=kc: e.tensor_scalar(
                        out=hT[:, kc, :], in0=tpb[:, kc, :], scalar1=scl[:, l, kc, r:r + 1],
                        scalar2=shf[:, l, kc, r:r + 1], op0=ALU.mult, op1=ALU.add),
                        reads=[btb, b_mod[l]], writes=[bh])
                else:
                    P.op("act", lambda e, kc=kc: e.activation(
                        out=hT[:, kc, :], in_=tpb[:, kc, :], func=AF.Identity,
                        bias=shf[:, l, kc, r:r + 1], scale=scl[:, l, kc, r:r + 1]),
                        reads=[btb, b_mod[l]], writes=[bh])
            return hT, bh, xt, bx

        def phaseP0(b):
            for t in range(NT):
                isx = t < NTX
                src = x_d[b, t * 128:(t + 1) * 128, :]
                hT, bh, _, _ = emit_hT(src, 0, b if isx else NB)
                qk, bqk = qk_r.next()
                qkf = qk.rearrange("p h d -> p (h d)")
                po, bpo = po_r.next()
                for gi, (c0, n) in enumerate(colgroups):
                    pb, bpb = pbank.next()
                    for kc in range(8):
                        P.op("pe", lambda e, pb=pb, kc=kc, c0=c0, n=n: e.matmul(
                            pb[:, 0:n], lhsT=hT[:, kc, :], rhs=Wb[:, kc, c0:c0 + n], start=(kc == 0), stop=(kc == 7)),
                            reads=[bh, b_Wb], writes=[bpb], signal=(kc == 7))
                    if gi < 4:
                        eng = "act" if gi % 2 == 0 else "dve"
                        if eng == "act":
                            P.op("act", lambda e, pb=pb, c0=c0, n=n: e.activation(out=qkf[:, c0:c0 + n], in_=pb[:, 0:n],
                                                                                  func=AF.Copy),
                                 reads=[bpb], writes=[bqk])
                        else:
                            P.op("dve", lambda e, pb=pb, c0=c0, n=n: e.tensor_copy(out=qkf[:, c0:c0 + n], in_=pb[:, 0:n]),
                                 reads=[bpb], writes=[bqk])
                    elif gi < 6:
                        P.op("dve", lambda e, pb=pb, c0=c0, n=n: e.tensor_copy(out=po[:, c0:c0 + n], in_=pb[:, 0:n]),
                             reads=[bpb], writes=[bpo])
                    else:
                        P.op("act", lambda e, pb=pb, c0=c0, n=n: e.activation(out=po[:, c0:c0 + n], in_=pb[:, 0:n],
                                                                              func=AF.Silu),
                             reads=[bpb], writes=[bpo])
                tA, bA = tA_r.next()
                tB, bB = tB_r.next()
                stt, bs = st_r.next()
                P.op("pool", lambda e: e.tensor_tensor(out=tA, in0=qk, in1=qk, op=ALU.mult), reads=[bqk], writes=[bA])
                P.op("dve", lambda e: e.tensor_reduce(out=stt[:, 0:26], in_=tA, axis=AXX, op=ALU.add),
                     reads=[bA], writes=[bs])
                P.op("dve", lambda e: e.tensor_scalar(out=stt[:, 0:26], in0=stt[:, 0:26], scalar1=1.0 / 64, scalar2=EPS,
                                                      op0=ALU.mult, op1=ALU.add), reads=[bs], writes=[bs])
                P.op("pool", lambda e: e.tensor_tensor(out=stt[:, 32:58], in0=stt[:, 0:26], in1=nhalf[:, 0:26],
                                                       op=ALU.pow), reads=[bs, b_cb], writes=[bs])
                P.op("dve", lambda e: e.tensor_tensor(
                    out=tB, in0=qk, in1=stt[:, 32:58].unsqueeze(2).broadcast_to([128, 26, 64]), op=ALU.mult),
                    reads=[bqk, bs], writes=[bB])
                pov = po[:, 0:1664].rearrange("p (h d) -> p h d", h=26)
                if isx:
                    P.op("pool", lambda e: e.tensor_tensor(out=tA, in0=tB, in1=gain_all, op=ALU.mult),
                         reads=[bB, b_par], writes=[bA])
                    tC, bC = tC_r.next()
                    P.op("pool", lambda e: e.tensor_tensor(
                        out=tB, in0=tA, in1=ropeC[:, t, :].unsqueeze(1).broadcast_to([128, 26, 64]), op=ALU.mult),
                        reads=[bA, b_rope], writes=[bB])
                    for blk in range(2):
                        for hf in range(2):
                            o0 = blk * 32 + hf * 16
                            i0 = blk * 32 + (1 - hf) * 16
                            P.op("dve", lambda e, o0=o0, i0=i0: e.tensor_tensor(
                                out=tC[:, :, o0:o0 + 16], in0=tA[:, :, i0:i0 + 16],
                                in1=ropeS[:, t, o0:o0 + 16].unsqueeze(1).broadcast_to([128, 26, 16]), op=ALU.mult),
                                reads=[bA, b_rope], writes=[bC])
                    P.op("dve", lambda e: e.tensor_tensor(out=pov, in0=tB, in1=tC, op=ALU.add),
                         reads=[bB, bC], writes=[bpo])
                else:
                    P.op("pool", lambda e: e.tensor_tensor(out=pov, in0=tB, in1=gain_all, op=ALU.mult),
                         reads=[bB, b_par], writes=[bpo])
                P.dma(pj_d[b, t * 128:(t + 1) * 128, :], po, bpo, reads=[bpo], writes=[b_pj[b][t]])

        for b in range(NB):
            phaseP0(b)
        P.barrier()
        AR.off = l0_mark

        Wob = AR.alloc([8, D], BF16)
        b_Wob = Buf("Wob")
        P.dma(Wob, wout_d[0].rearrange("(kc p) n -> p kc n", p=128), b_Wob, writes=[b_Wob], queue="pool")
        b_kraw = Buf("kraw")
        kAT = AR.alloc([2, NT * 128], BF16)
        kBT = AR.alloc([4, NT * 128], BF16)
        b_kT = Buf("kT")
        vA = AR.alloc([NT, 2, 66], BF16)
        vB = AR.alloc([NT, 4, 130], BF16)
        b_vA = Buf("vA")
        b_vB = Buf("vB")
        gbc = AR.alloc([2, D], F32)
        b_gbc = Buf("gbc")
        qg_r = mkring(AR, "qg", 2, [2, 2048], BF16)
        qAT_r = mkring(AR, "qAT", 2, [8, 256], BF16)
        qBT_r = mkring(AR, "qBT", 2, [4, 256], BF16)
        _m = AR.off
        kraw = AR.alloc([NT, 640], BF16)
        AR.off = _m
        PA_r = Ring([(AR.alloc([5, 512], BF16), [Buf("PA%d_%d" % (i, k)) for k in range(5)]) for i in range(2)])
        PB_r = Ring([(AR.alloc([NT, 256], BF16), [Buf("PB%d_%d" % (i, k)) for k in range(9)]) for i in range(2)])
        pall = [bb_ for (_, bl) in PA_r.items for bb_ in bl] + [bb_ for (_, bl) in PB_r.items for bb_ in bl]
        otok_r = mkring(AR, "otok", 2, [D], BF16)
        dB_r = mkring(AR, "dB", 2, [4, 128], F32)
        oT_r = mkring(AR, "oT", 2, [8, 128], BF16)
        xr_r = mkring(AR, "xr", 2, [D], F32)
        ty_r = mkring(AR, "ty", 2, [512], F32)
        tmp_r = mkring(AR, "tmp", 3, [512], F32)
        s2_r = mkring(AR, "s2", 4, [32], F32)
        sbank = Ring([banks[0], banks[1], banks[2]])
        pvbank_A = Ring([banks[3], banks[4]])
        T0, T1, YB = banks[5], banks[6], banks[7]
        b_x1 = [[Buf("x1_%d_%d" % (b, t)) for t in range(NT)] for b in range(NB)]
        pjv = [pj_d[b].rearrange("(t p) c -> p t c", p=128) for b in range(NB)]

        def phaseATT0(b):
            for ri, r in enumerate((b, NB)):
                for cg in range(2):
                    yb, byb = YB
                    P.op("pe", lambda e, r=r, cg=cg, yb=yb: e.matmul(
                        yb, lhsT=cst[0:8, C_SEL + r * 128:C_SEL + (r + 1) * 128],
                        rhs=msbg[0:8, 0, cg * 512:(cg + 1) * 512], start=True, stop=True),
                        reads=[b_cst, b_msb[0]], writes=[byb])
                    P.op("dve", lambda e, ri=ri, cg=cg, yb=yb: e.tensor_copy(out=gbc[:, ri, cg * 512:(cg + 1) * 512], in_=yb),
                         reads=[byb], writes=[b_gbc])
            P.dma(kraw[:, :, 0:128], pjv[b][:, :, 512:640], b_kraw, reads=b_pj[b], writes=[b_kraw] + pall)
            P.dma(kraw[:, :, 128:640], pjv[b][:, :, 1152:1664], b_kraw, reads=b_pj[b], writes=[b_kraw] + pall)
            for t in range(NT):
                P.dma(vB[:, t, :, 0:128], pjv[b][:, t, 1664:2176].rearrange("p (h d) -> p h d", h=4),
                      b_vB, reads=[b_pj[b][t]], writes=[b_vB])
                P.dma(vA[:, t, :, 0:64], pjv[b][:, t, 2176:2304].rearrange("p (h d) -> p h d", h=2),
                      b_vA, reads=[b_pj[b][t]], writes=[b_vA])
            P.op("pool", lambda e: e.memset(vA[:, :, :, 64:66], 1.0), writes=[b_vA])
            P.op("pool", lambda e: e.memset(vB[:, :, :, 128:130], 1.0), writes=[b_vB])
            for t in range(NT):
                tb, btb = T0 if t % 2 == 0 else T1
                tv = tb.bitcast(BF16)
                ta = tv[0:64, 0:256].rearrange("p (h t) -> p h t", h=2)
                tbb = tv[:, 256:768].rearrange("p (h t) -> p h t", h=4)
                for kv in range(2):
                    P.op("pe", lambda e, kv=kv, ta=ta: e.transpose(out=ta[:, kv, :], in_=kraw[:, t, kv * 64:(kv + 1) * 64],
                                                                   identity=identb),
                         reads=[b_kraw, b_cb], writes=[btb], signal=False)
                for h in range(4):
                    P.op("pe", lambda e, h=h, tbb=tbb: e.transpose(
                        out=tbb[:, h, :], in_=kraw[:, t, 128 + h * 128:128 + (h + 1) * 128], identity=identb),
                        reads=[b_kraw, b_cb], writes=[btb], signal=(h == 3))
                P.op("act", lambda e, ta=ta: e.activation(out=kAT[0:64, :, t * 128:(t + 1) * 128], in_=ta, func=AF.Copy),
                     reads=[btb], writes=[b_kT])
                P.op("dve", lambda e, tbb=tbb: e.tensor_copy(out=kBT[:, :, t * 128:(t + 1) * 128], in_=tbb),
                     reads=[btb], writes=[b_kT])

            for G in range(9):
                isx = G < 8
                tiles = (2 * G, 2 * G + 1)
                ri = 0 if isx else 1
                qg, bqg = qg_r.next()
                tsl = slice(2 * G, 2 * G + 2)
                rd = [b_pj[b][tiles[0]], b_pj[b][tiles[1]]]
                P.dma(qg[:, :, 0:512], pjv[b][:, tsl, 0:512], bqg, reads=rd, writes=[bqg])
                P.dma(qg[:, :, 512:1024], pjv[b][:, tsl, 640:1152], bqg, reads=rd, writes=[bqg])
                P.dma(qg[:, :, 1024:2048], pjv[b][:, tsl, 2304:3328], bqg, reads=rd, writes=[bqg])
                qAT, bqa = qAT_r.next()
                qBT, bqb = qBT_r.next()
                for j in range(2):
                    tb0, bt0 = T0
                    tb1, bt1 = T1
                    ta = tb0.bitcast(BF16)[0:64, :].rearrange("p (h t) -> p h t", h=8)
                    tbb = tb1.bitcast(BF16)[:, 0:512].rearrange("p (h t) -> p h t", h=4)
                    for h in range(8):
                        P.op("pe", lambda e, h=h, j=j, ta=ta: e.transpose(
                            out=ta[:, h, :], in_=qg[:, j, h * 64:(h + 1) * 64], identity=identb),
                            reads=[bqg, b_cb], writes=[bt0], signal=(h == 7))
                    for h in range(4):
                        P.op("pe", lambda e, h=h, j=j, tbb=tbb: e.transpose(
                            out=tbb[:, h, :], in_=qg[:, j, 512 + h * 128:512 + (h + 1) * 128], identity=identb),
                            reads=[bqg, b_cb], writes=[bt1], signal=(h == 3))
                    P.op("act", lambda e, j=j, ta=ta: e.activation(out=qAT[0:64, :, j * 128:(j + 1) * 128], in_=ta,
                                                                   func=AF.Copy), reads=[bt0], writes=[bqa])
                    P.op("dve", lambda e, j=j, tbb=tbb: e.tensor_copy(out=qBT[:, :, j * 128:(j + 1) * 128], in_=tbb),
                         reads=[bt1], writes=[bqb])
                otoks = [otok_r.next() for _ in range(2)]
                for j in range(2):
                    ti = tiles[j] if isx else NTX + j
                    if isx:
                        kts = []
                        if ti > 0:
                            kts.append((ti - 1, 0))
                        kts.append((ti, None))
                        if ti < NTX - 1:
                            kts.append((ti + 1, 1))
                        kts += [(NTX, None), (NTX + 1, None)]
                    else:
                        kts = [(NTX, None), (NTX + 1, None)]
                    otok, bot = otoks[j]
                    for kv in range(2):
                        PA, bPA = PA_r.next()
                        for idx, (kt, mk) in enumerate(kts):
                            sb, bsb = sbank.next()
                            P.op("pe", lambda e, sb=sb, kv=kv, kt=kt, j=j: e.matmul(
                                sb, lhsT=kAT[0:64, kv, kt * 128:(kt + 1) * 128],
                                rhs=qAT[0:64, kv * 4:(kv + 1) * 4, j * 128:(j + 1) * 128], start=True, stop=True),
                                reads=[b_kT, bqa], writes=[bsb])
                            P.op("act", lambda e, sb=sb, idx=idx, PA=PA: e.activation(
                                out=PA[:, idx, :], in_=sb, func=AF.Exp, scale=0.125), reads=[bsb], writes=[bPA[idx]])
                            if mk is not None:
                                P.op("pool", lambda e, idx=idx, PA=PA, mk=mk: e.tensor_tensor(
                                    out=PA[:, idx, :], in0=PA[:, idx, :],
                                    in1=maskb[:, mk].rearrange("p h q -> p (h q)"), op=ALU.mult),
                                    reads=[bPA[idx], b_cb], writes=[bPA[idx]])
                        pvb, bpv = pvbank_A.next()
                        pvv = pvb[:, 0:4 * 66].rearrange("p (g d) -> p g d", g=4)
                        for g in range(4):
                            for idx, (kt, mk) in enumerate(kts):
                                P.op("pe", lambda e, g=g, idx=idx, kt=kt, kv=kv, PA=PA, pvv=pvv: e.matmul(
                                    pvv[:, g, 0:65], lhsT=PA[:, idx, g * 128:(g + 1) * 128], rhs=vA[:, kt, kv, 0:65],
                                    start=(idx == 0), stop=(idx == len(kts) - 1)),
                                    reads=[bPA[idx], b_vA], writes=[bpv], signal=(idx == len(kts) - 1))
                        s2, bs2 = s2_r.next()
                        tmp, btm = tmp_r.next()
                        P.op("dve", lambda e, pvv=pvv, s2=s2, kv=kv: e.tensor_tensor(
                            out=s2[:, 0:4], in0=pvv[:, :, 64], in1=esink[:, kv * 4:(kv + 1) * 4], op=ALU.add),
                            reads=[bpv, b_par], writes=[bs2])
                        P.op("dve", lambda e, s2=s2: e.reciprocal(out=s2[:, 4:8], in_=s2[:, 0:4]), reads=[bs2], writes=[bs2])
                        tmpv = tmp[:, 0:256].rearrange("p (g d) -> p g d", g=4)
                        P.op("dve", lambda e, pvv=pvv, s2=s2, tmpv=tmpv: e.tensor_tensor(
                            out=tmpv, in0=pvv[:, :, 0:64], in1=s2[:, 4:8].unsqueeze(2).broadcast_to([128, 4, 64]),
                            op=ALU.mult), reads=[bpv, bs2], writes=[btm])
                        P.op("pool", lambda e, tmp=tmp, otok=otok, kv=kv, j=j: e.tensor_tensor(
                            out=otok[:, kv * 256:(kv + 1) * 256], in0=tmp[:, 0:256],
                            in1=qg[:, j, 1024 + kv * 256:1024 + (kv + 1) * 256], op=ALU.mult),
                            reads=[btm, bqg], writes=[bot])
                pairs = [(2 * i, 2 * i + 1) for i in range(9)] if isx else [(NTX, NTX + 1)]
                dBs = [dB_r.next() for _ in range(2)]
                for h in range(4):
                    pvj = [banks[3], banks[4]]
                    for s in range(2):
                        PB, bPB = PB_r.next()
                        for pi, pr in enumerate(pairs):
                            sb, bsb = sbank.next()
                            sbv = sb.rearrange("p (u q) -> p u q", u=2)
                            for u in range(2):
                                kt = pr[u]
                                P.op("pe", lambda e, sbv=sbv, u=u, kt=kt, s=s, h=h: e.matmul(
                                    sbv[:, u, :], lhsT=kBT[s * 64:(s + 1) * 64, h, kt * 128:(kt + 1) * 128],
                                    rhs=qBT[s * 64:(s + 1) * 64, h, :], start=True, stop=True),
                                    reads=[b_kT, bqb], writes=[bsb], signal=(u == 1))
                            P.op("act", lambda e, sbv=sbv, pi=pi, PB=PB: e.activation(
                                out=PB[:, 2 * pi:2 * pi + 2, :], in_=sbv, func=AF.Exp, scale=0.125),
                                reads=[bsb], writes=[bPB[pi]])
                        for j in range(2):
                            pvb, bpv = pvj[j]
                            pvv = pvb[:, 0:260].rearrange("p (s d) -> p s d", s=2)
                            nk = len(pairs) * 2
                            for pi, pr in enumerate(pairs):
                                for u in range(2):
                                    kt = pr[u]
                                    ki = 2 * pi + u
                                    P.op("pe", lambda e, pvv=pvv, s=s, ki=ki, kt=kt, j=j, PB=PB, h=h: e.matmul(
                                        pvv[:, s, 0:129], lhsT=PB[:, ki, j * 128:(j + 1) * 128], rhs=vB[:, kt, h, 0:129],
                                        start=(ki == 0), stop=(ki == nk - 1)),
                                        reads=[bPB[pi], b_vB], writes=[bpv], signal=(ki == nk - 1))
                    for j in range(2):
                        pvb, bpv = pvj[j]
                        pvv = pvb[:, 0:260].rearrange("p (s d) -> p s d", s=2)
                        dB, bdB = dBs[j]
                        s2, bs2 = s2_r.next()
                        tmp, btm = tmp_r.next()
                        P.op("dve", lambda e, pvv=pvv, s2=s2: e.reciprocal(out=s2[:, 0:2], in_=pvv[:, :, 128]),
                             reads=[bpv], writes=[bs2])
                        P.op("dve", lambda e, s2=s2: e.tensor_tensor(out=s2[:, 2:3], in0=s2[:, 1:2], in1=neglam, op=ALU.mult),
                             reads=[bs2, b_par], writes=[bs2])
                        P.op("dve", lambda e, pvv=pvv, s2=s2, tmp=tmp: e.tensor_scalar(
                            out=tmp[:, 0:128], in0=pvv[:, 1, 0:128], scalar1=s2[:, 2:3], scalar2=None, op0=ALU.mult),
                            reads=[bpv, bs2], writes=[btm])
                        P.op("dve", lambda e, pvv=pvv, s2=s2, tmp=tmp, dB=dB, h=h: e.scalar_tensor_tensor(
                            out=dB[:, h, :], in0=pvv[:, 0, 0:128], scalar=s2[:, 0:1], in1=tmp[:, 0:128],
                            op0=ALU.mult, op1=ALU.add), reads=[bpv, bs2, btm], writes=[bdB])
                for j in range(2):
                    ti = tiles[j] if isx else NTX + j
                    otok, bot = otoks[j]
                    dB, bdB = dBs[j]
                    s2, bs2 = s2_r.next()
                    tmp, btm = tmp_r.next()
                    tmp2, btm2 = tmp_r.next()
                    tv = tmp.rearrange("p (h d) -> p h d", h=4)
                    tv2 = tmp2.rearrange("p (h d) -> p h d", h=4)
                    P.op("pool", lambda e, tv=tv, dB=dB: e.tensor_tensor(out=tv, in0=dB, in1=dB, op=ALU.mult),
                         reads=[bdB], writes=[btm])
                    P.op("dve", lambda e, tv=tv, s2=s2: e.tensor_reduce(out=s2[:, 0:4], in_=tv, axis=AXX, op=ALU.add),
                         reads=[btm], writes=[bs2])
                    P.op("dve", lambda e, s2=s2: e.tensor_scalar(out=s2[:, 0:4], in0=s2[:, 0:4], scalar1=1.0 / 128,
                                                                 scalar2=EPS, op0=ALU.mult, op1=ALU.add),
                         reads=[bs2], writes=[bs2])
                    P.op("pool", lambda e, s2=s2: e.tensor_tensor(out=s2[:, 4:8], in0=s2[:, 0:4], in1=nhalf[:, 0:4], op=ALU.pow),
                         reads=[bs2, b_cb], writes=[bs2])
                    P.op("dve", lambda e, tv=tv, dB=dB, s2=s2: e.tensor_tensor(
                        out=tv, in0=dB, in1=s2[:, 4:8].unsqueeze(2).broadcast_to([128, 4, 128]), op=ALU.mult),
                        reads=[bdB, bs2], writes=[btm])
                    P.op("pool", lambda e, tv=tv, tv2=tv2: e.tensor_tensor(
                        out=tv2, in0=tv, in1=subln8.unsqueeze(1).broadcast_to([128, 4, 128]), op=ALU.mult),
                        reads=[btm, b_par], writes=[btm2])
                    P.op("pool", lambda e, tmp2=tmp2, otok=otok, j=j: e.tensor_tensor(
                        out=otok[:, 512:1024], in0=tmp2, in1=qg[:, j, 1536:2048], op=ALU.mult),
                        reads=[btm2, bqg], writes=[bot])
                    emit_outproj(b, ti, otok, bot, ri, isx, 0)

        def emit_outproj(b, ti, otok, bot, ri, isx, l):
            tb0, bt0 = T0
            tp = tb0.bitcast(BF16).rearrange("p (k t) -> p k t", k=8)
            for kc in range(8):
                P.op("pe", lambda e, kc=kc: e.transpose(out=tp[:, kc, :], in_=otok[:, kc * 128:(kc + 1) * 128],
                                                        identity=identb),
                     reads=[bot, b_cb], writes=[bt0], signal=(kc == 7))
            oT, boT = oT_r.next()
            P.op("act", lambda e: e.activation(out=oT, in_=tp, func=AF.Copy), reads=[bt0], writes=[boT])
            xr, bxr = xr_r.next()
            if l == 0:
                src = x_d[b, ti * 128:(ti + 1) * 128, :]
                P.dma(xr, src, bxr, writes=[bxr])
            else:
                P.dma(xr, out_d[b, ti * 128:(ti + 1) * 128, :], bxr, reads=[b_x1[b][ti]], writes=[bxr])
            xo, bxo = xr, bxr
            for cg in range(2):
                yb, byb = YB
                for kc in range(8):
                    P.op("pe", lambda e, kc=kc, cg=cg, yb=yb: e.matmul(
                        yb, lhsT=oT[:, kc, :], rhs=Wob[:, kc, cg * 512:(cg + 1) * 512], start=(kc == 0), stop=(kc == 7)),
                        reads=[boT, b_Wob], writes=[byb], signal=(kc == 7))
                ty, bty = ty_r.next()
                P.op("dve", lambda e, yb=yb, ty=ty, cg=cg: e.tensor_tensor(
                    out=ty, in0=yb, in1=gbc[:, ri, cg * 512:(cg + 1) * 512], op=ALU.mult),
                    reads=[byb, b_gbc], writes=[bty])
                P.op("pool", lambda e, ty=ty, cg=cg: e.tensor_tensor(
                    out=xo[:, cg * 512:(cg + 1) * 512], in0=ty, in1=xr[:, cg * 512:(cg + 1) * 512], op=ALU.add),
                    reads=[bty, bxr], writes=[bxo])
            dst = out_d[b, ti * 128:(ti + 1) * 128, :]
            if l == 0:
                P.dma(dst, xo, bxo, reads=[bxo], writes=[b_x1[b][ti]])
            else:
                P.dma(dst, xo, bxo, reads=[bxo, b_x1[b][ti]], writes=[b_x1[b][ti]])

        for b in range(NB):
            phaseATT0(b)
        P.barrier()
        AR.off = l0_mark

        if stage >= 2:
            la_d = nc.dram_tensor("la", [NB, NT * 128, 1024], F32, kind="Internal").ap()
            sb_d = nc.dram_tensor("sbst", [NB, 32, 128, 1024], BF16, kind="Internal").ap()
            QS = 128.0 ** -0.5
            Wb1 = AR.alloc([8, O_IN], BF16)
            b_Wb1 = Buf("Wb1")
            w1v = win1_d.rearrange("(kc p) n -> p kc n", p=128)
            for c0 in range(0, O_IN, 512):
                n = min(512, O_IN - c0)
                P.dma(Wb1[:, :, c0:c0 + n], w1v[:, :, c0:c0 + n], b_Wb1, writes=[b_Wb1], queue="pool")
            waf = AR.alloc([2, 512], F32)
            b_wa = Buf("wa")
            P.dma(waf[0:16, 0, :], waf_d, b_wa, writes=[b_wa])
            P.dma(waf[0:16, 1, :], wab_d, b_wa, writes=[b_wa])
            barow = AR.alloc([2, 512], F32, parts=1)
            P.dma(barow[0:1, 0, :], baf_d, b_wa, writes=[b_wa])
            P.dma(barow[0:1, 1, :], bab_d, b_wa, writes=[b_wa])
            ones128 = AR.alloc([128], F32, parts=1)
            P.op("pool", lambda e: e.memset(ones128, 1.0), writes=[b_wa])
            xt_r = mkring(AR, "xt1", 2, [D], F32)
            junk = AR.alloc([D], BF16)
            xn_r = mkring(AR, "xn1", 2, [D], BF16)
            hT_r = mkring(AR, "hT1", 2, [8, 128], BF16)
            st_r = mkring(AR, "stat1", 4, [64], F32)
            po_r = mkring(AR, "po1", 2, [3072], BF16)
            rr_r = mkring(AR, "rr", 2, [32], F32)
            rT_r = mkring(AR, "rT", 2, [2, 128], F32)
            la_r = mkring(AR, "las", 2, [1024], F32)
            et_r = mkring(AR, "et", 2, [1024], F32)
            pbank = Ring([banks[0], banks[1]])
            tbank = Ring([banks[2], banks[3]])
            b_pj1 = [[Buf("pj1_%d_%d" % (b, t)) for t in range(NT)] for b in range(NB)]
            b_la = [[Buf("la_%d_%d" % (b, t)) for t in range(NT)] for b in range(NB)]
            cg1 = [(c0, min(512, O_IN - c0)) for c0 in range(0, O_IN, 512)]

            def phaseP1(b):
                for t in range(NT):
                    isx = t < NTX
                    src = out_d[b, t * 128:(t + 1) * 128, :]
                    hT, bh, _, _ = emit_hT(src, 1, b if isx else NB, b_src_dram=b_x1[b][t])
                    po, bpo = po_r.next()
                    rr, brr = rr_r.next()
                    for gi, (c0, n) in enumerate(cg1):
                        pb, bpb = pbank.next()
                        for kc in range(8):
                            P.op("pe", lambda e, pb=pb, kc=kc, c0=c0, n=n: e.matmul(
                                pb[:, 0:n], lhsT=hT[:, kc, :], rhs=Wb1[:, kc, c0:c0 + n], start=(kc == 0), stop=(kc == 7)),
                                reads=[bh, b_Wb1], writes=[bpb], signal=(kc == 7))
                        if gi == 0:
                            P.op("act", lambda e, pb=pb: e.activation(out=po[:, 0:512], in_=pb, func=AF.Copy, scale=QS),
                                 reads=[bpb], writes=[bpo])
                        elif gi in (1, 3):
                            P.op("dve", lambda e, pb=pb, c0=c0: e.tensor_copy(out=po[:, c0:c0 + 512], in_=pb),
                                 reads=[bpb], writes=[bpo])
                        elif gi == 2:
                            P.op("act", lambda e, pb=pb, c0=c0: e.activation(out=po[:, c0:c0 + 512], in_=pb, func=AF.Copy),
                                 reads=[bpb], writes=[bpo])
                        elif gi in (4, 5):
                            P.op("act", lambda e, pb=pb, c0=c0: e.activation(out=po[:, c0:c0 + 512], in_=pb, func=AF.Silu),
                                 reads=[bpb], writes=[bpo])
                        else:
                            P.op("dve", lambda e, pb=pb: e.tensor_copy(out=rr, in_=pb[:, 0:32]), reads=[bpb], writes=[brr])
                    P.dma(pj_d[b, t * 128:(t + 1) * 128, 0:3072], po, bpo, reads=[bpo], writes=[b_pj1[b][t]])
                    tb, btb = tbank.next()
                    rT, brT = rT_r.next()
                    for dr in range(2):
                        P.op("pe", lambda e, dr=dr, tb=tb: e.transpose(out=tb[0:16, dr * 128:(dr + 1) * 128],
                                                                      in_=rr[:, dr * 16:(dr + 1) * 16],
                                                                      identity=cst[:, C_ID:C_ID + 128]),
                             reads=[brr, b_cst], writes=[btb], signal=(dr == 1))
                    P.op("dve", lambda e, tb=tb: e.tensor_copy(out=rT[0:16], in_=tb[0:16, 0:256].rearrange("p (a t) -> p a t", a=2)),
                         reads=[btb], writes=[brT])
                    las, bls = la_r.next()
                    et, bet = et_r.next()
                    for dr in range(2):
                        pb, bpb = pbank.next()
                        P.op("pe", lambda e, dr=dr, pb=pb: e.matmul(pb, lhsT=rT[0:16, dr, :], rhs=waf[0:16, dr, :],
                                                                   start=True, stop=False),
                             reads=[brT, b_wa], writes=[bpb], signal=False)
                        P.op("pe", lambda e, dr=dr, pb=pb: e.matmul(pb, lhsT=ones128[0:1, :], rhs=barow[0:1, dr, :],
                                                                   start=False, stop=True),
                             reads=[b_wa], writes=[bpb])
                        P.op("act", lambda e, dr=dr, pb=pb: e.activation(out=et[:, dr * 512:(dr + 1) * 512], in_=pb,
                                                                        func=AF.Exp, scale=-1.0),
                             reads=[bpb], writes=[bet])
                    P.op("act", lambda e: e.activation(out=et, in_=et, func=AF.Ln, bias=1.0), reads=[bet], writes=[bet])
                    P.op("pool", lambda e: e.tensor_scalar(out=las, in0=et, scalar1=-1.0 / 16.0, scalar2=None, op0=ALU.mult),
                         reads=[bet], writes=[bls])
                    P.dma(la_d[b, t * 128:(t + 1) * 128, :], las, bls, reads=[bls], writes=[b_la[b][t]])

            for b in range(NB):
                phaseP1(b)
            P.barrier()
            AR.off = l0_mark

            Wob = AR.alloc([8, D], BF16)
            b_Wob = Buf("Wob1")
            P.dma(Wob, wout_d[1].rearrange("(kc p) n -> p kc n", p=128), b_Wob, writes=[b_Wob], queue="pool")
            gbc = AR.alloc([1, D], F32)
            b_gbc = Buf("gbc1")
            onbc = AR.alloc([4, 256], F32)
            b_on = Buf("onbc")
            P.op("dve", lambda e: e.tensor_copy(out=onbc, in_=pv[:, PV_ONORM:PV_ONORM + 256].unsqueeze(1).broadcast_to([128, 4, 256])),
                 reads=[b_pv], writes=[b_on])
            tri = cst[:, C_TRI:C_TRI + 512].rearrange("p (m c) -> p m c", m=4)
            ind = cst[:, C_IND:C_IND + 2]
            kb_r = mkring(AR, "kb", 2, [512], BF16)
            vb_r = mkring(AR, "vb", 2, [1024], BF16)
            lb_r = mkring(AR, "lb", 2, [512], F32)
            eb_r = mkring(AR, "eb", 2, [512], F32)
            kdb_r = mkring(AR, "kdb", 2, [512], BF16)
            decb_r = mkring(AR, "decb", 2, [4, 2], F32)
            Sb = AR.alloc([4, 256], F32)
            b_Sb = Buf("Sb")
            sbo_r = mkring(AR, "sbo", 2, [1024], BF16)
            b_sbd = [[Buf("sbd%d_%d" % (b, c)) for c in range(32)] for b in range(NB)]
            qf_r = mkring(AR, "qf", 2, [512], BF16)
            kf_r = mkring(AR, "kf", 2, [512], BF16)
            vf_r = mkring(AR, "vf", 2, [1024], BF16)
            gf_r = mkring(AR, "gf", 2, [1024], BF16)
            lf_r = mkring(AR, "lf", 2, [1024], F32)
            E_r = mkring(AR, "E", 1, [5, 512], F32)
            pr_r = mkring(AR, "pr", 2, [5, 512], BF16)
            tr_r = mkring(AR, "tr", 2, [4, 4, 128], BF16)
            at_r = mkring(AR, "at", 2, [2, 4, 128], BF16)
            decf_r = mkring(AR, "decf", 2, [4, 2], F32)
            Sf = AR.alloc([4, 256], F32)
            b_Sf = Buf("Sf")
            sfb_r = mkring(AR, "sfb", 4, [1024], BF16)
            sbl_r = mkring(AR, "sbl", 2, [2, 1024], BF16)
            of_r = mkring(AR, "of", 1, [4, 256], F32)
            oq_r = mkring(AR, "oq", 1, [4, 256], F32)
            otok_r = mkring(AR, "otok1", 2, [D], BF16)
            oT_r = mkring(AR, "oT1", 2, [8, 128], BF16)
            xr_r = mkring(AR, "xr1", 2, [D], F32)
            ty_r = mkring(AR, "ty1", 2, [512], F32)
            s2_r = mkring(AR, "s21", 4, [32], F32)
            bring = Ring(banks[0:4])
            pring = Ring([(banks[4], banks[5]), (banks[6], banks[7])])
            T0, YB = banks[2], banks[3]

            def pair_ap(pr):
                i = int(pr[0][1].name[4:])
                return psum_t[:, i:i + 2, :].rearrange("p a c -> p (a c)")

            def emit_gbc1(b):
                for cg in range(2):
                    yb, byb = YB
                    P.op("pe", lambda e, cg=cg, yb=yb: e.matmul(
                        yb, lhsT=cst[0:8, C_SEL + b * 128:C_SEL + (b + 1) * 128],
                        rhs=msbg[0:8, 1, cg * 512:(cg + 1) * 512], start=True, stop=True),
                        reads=[b_cst, b_msb[1]], writes=[byb])
                    P.op("dve", lambda e, cg=cg, yb=yb: e.tensor_copy(out=gbc[:, 0, cg * 512:(cg + 1) * 512], in_=yb),
                         reads=[byb], writes=[b_gbc])

            def tri_mm(m, la_ap, bla):
                bk, bb = bring.next()
                P.op("pe", lambda e: e.matmul(bk, lhsT=tri[:, m, :], rhs=la_ap, start=True, stop=True),
                     reads=[b_cst, bla], writes=[bb])
                return bk, bb

            def dec_mm(la_ap, bla, dec, bdec):
                bk, bb = bring.next()
                for h in range(4):
                    P.op("pe", lambda e, h=h: e.matmul(bk[:, 2 * h:2 * h + 2], lhsT=la_ap[:, h * 128:(h + 1) * 128], rhs=ind,
                                                       start=True, stop=True),
                         reads=[b_cst, bla], writes=[bb], signal=(h == 3))
                P.op("act", lambda e: e.activation(out=dec, in_=bk[:, 0:8].rearrange("p (h c) -> p h c", h=4), func=AF.Exp),
                     reads=[bb], writes=[bdec])

            def phaseB1(b):
                P.op("pool", lambda e: e.memset(Sb, 0.0), writes=[b_Sb])
                for t in list(range(NT - 1, NTX - 1, -1)) + list(range(NTX - 1, -1, -1)):
                    isx = t < NTX
                    rows = slice(t * 128, (t + 1) * 128)
                    kk, bkk = kb_r.next()
                    vv, bvv = vb_r.next()
                    lb, blb = lb_r.next()
                    P.dma(kk, pj_d[b, rows, 512:1024], bkk, reads=[b_pj1[b][t]], writes=[bkk])
                    P.dma(vv, pj_d[b, rows, 1024:2048], bvv, reads=[b_pj1[b][t]], writes=[bvv])
                    P.dma(lb, la_d[b, rows, 512:1024], blb, reads=[b_la[b][t]], writes=[blb])
                    bk, bb = tri_mm(3, lb, blb)
                    eb, beb = eb_r.next()
                    P.op("act", lambda e: e.activation(out=eb, in_=bk, func=AF.Exp), reads=[bb], writes=[beb])
                    dec, bdec = decb_r.next()
                    dec_mm(lb, blb, dec, bdec)
                    kd, bkd = kdb_r.next()
                    P.op("pool", lambda e: e.tensor_tensor(out=kd, in0=kk, in1=eb, op=ALU.mult), reads=[bkk, beb], writes=[bkd])
                    for ci in (1, 0):
                        cr = slice(ci * 64, (ci + 1) * 64)
                        pr = pring.next()
                        pap = pair_ap(pr).rearrange("p (h e) -> p h e", h=4)
                        for h in range(4):
                            P.op("pe", lambda e, h=h, cr=cr, pap=pap: e.matmul(
                                pap[:, h, :], lhsT=kd[cr, h * 128:(h + 1) * 128], rhs=vv[cr, h * 256:(h + 1) * 256],
                                start=True, stop=True), reads=[bkd, bvv], writes=[pr[0][1], pr[1][1]], signal=(h == 3))
                        if isx:
                            so, bso = sbo_r.next()
                            P.op("act", lambda e, so=so: e.activation(out=so, in_=Sb.rearrange("p h e -> p (h e)"), func=AF.Copy),
                                 reads=[b_Sb], writes=[bso])
                            cidx = t * 2 + ci
                            P.dma(sb_d[b, cidx], so, bso, reads=[bso], writes=[b_sbd[b][cidx]])
                        for h in range(4):
                            P.op("dve", lambda e, h=h, ci=ci, pap=pap: e.scalar_tensor_tensor(
                                out=Sb[:, h, :], in0=Sb[:, h, :], scalar=dec[:, h, ci:ci + 1], in1=pap[:, h, :],
                                op0=ALU.mult, op1=ALU.add), reads=[b_Sb, bdec, pr[0][1], pr[1][1]], writes=[b_Sb])

            def phaseF1(b):
                emit_gbc1(b)
                P.op("pool", lambda e: e.memset(Sf, 0.0), writes=[b_Sf])
                for t in [NTX, NTX + 1] + list(range(NTX)):
                    isx = t < NTX
                    rows = slice(t * 128, (t + 1) * 128)
                    rd = [b_pj1[b][t]]
                    kk, bkk = kf_r.next()
                    vv, bvv = vf_r.next()
                    lf, blf = lf_r.next()
                    P.dma(kk, pj_d[b, rows, 512:1024], bkk, reads=rd, writes=[bkk])
                    P.dma(vv, pj_d[b, rows, 1024:2048], bvv, reads=rd, writes=[bvv])
                    P.dma(lf, la_d[b, rows, :], blf, reads=[b_la[b][t]], writes=[blf])
                    E, bE = E_r.next()
                    pr5, bpr = pr_r.next()
                    dec, bdec = decf_r.next()
                    bk, bb = tri_mm(1, lf[:, 0:512], blf)
                    P.op("act", lambda e, bk=bk: e.activation(out=E[:, 2, :], in_=bk, func=AF.Exp), reads=[bb], writes=[bE])
                    dec_mm(lf[:, 0:512], blf, dec, bdec)
                    P.op("pool", lambda e: e.tensor_tensor(out=pr5[:, 2, :], in0=kk, in1=E[:, 2, :], op=ALU.mult),
                         reads=[bkk, bE], writes=[bpr])
                    if isx:
                        qq, bqq = qf_r.next()
                        gg, bgg = gf_r.next()
                        P.dma(qq, pj_d[b, rows, 0:512], bqq, reads=rd, writes=[bqq])
                        P.dma(gg, pj_d[b, rows, 2048:3072], bgg, reads=rd, writes=[bgg])
                        sbl, bsbl = sbl_r.next()
                        P.dma(sbl, sb_d[b, 2 * t:2 * t + 2].rearrange("c p n -> p c n"), bsbl,
                              reads=[b_sbd[b][2 * t], b_sbd[b][2 * t + 1]], writes=[bsbl])
                        bk, bb = tri_mm(0, lf[:, 0:512], blf)
                        P.op("act", lambda e, bk=bk: e.activation(out=E[:, 0, :], in_=bk, func=AF.Exp), reads=[bb], writes=[bE])
                        P.op("act", lambda e, bk=bk: e.activation(out=E[:, 1, :], in_=bk, func=AF.Exp, scale=-1.0),
                             reads=[bb], writes=[bE])
                        bk, bb = tri_mm(2, lf[:, 512:1024], blf)
                        P.op("act", lambda e, bk=bk: e.activation(out=E[:, 3, :], in_=bk, func=AF.Exp), reads=[bb], writes=[bE])
                        P.op("act", lambda e, bk=bk: e.activation(out=E[:, 4, :], in_=bk, func=AF.Exp, scale=-1.0),
                             reads=[bb], writes=[bE])
                        for (slot, src, bsrc, ei, eng) in ((0, qq, bqq, 0, "dve"), (1, kk, bkk, 1, "pool"),
                                                           (3, qq, bqq, 3, "dve"), (4, kk, bkk, 4, "pool")):
                            P.op(eng, lambda e, slot=slot, src=src, ei=ei: e.tensor_tensor(
                                out=pr5[:, slot, :], in0=src, in1=E[:, ei, :], op=ALU.mult),
                                reads=[bsrc, bE], writes=[bpr])
                        trt, btr = tr_r.next()
                        for wi, slot in enumerate((0, 1, 3, 4)):
                            tb, btb = bring.next()
                            tbv = tb.bitcast(BF16)[:, 0:512].rearrange("p (h t) -> p h t", h=4)
                            for h in range(4):
                                P.op("pe", lambda e, h=h, slot=slot, tbv=tbv: e.transpose(
                                    out=tbv[:, h, :], in_=pr5[:, slot, h * 128:(h + 1) * 128], identity=identb),
                                    reads=[bpr, b_cb], writes=[btb], signal=(h == 3))
                            if wi % 2 == 0:
                                P.op("act", lambda e, wi=wi, tbv=tbv: e.activation(out=trt[:, wi], in_=tbv, func=AF.Copy),
                                     reads=[btb], writes=[btr])
                            else:
                                P.op("dve", lambda e, wi=wi, tbv=tbv: e.tensor_copy(out=trt[:, wi], in_=tbv),
                                     reads=[btb], writes=[btr])
                        att, bat = at_r.next()
                        for di in range(2):
                            ab, bab = bring.next()
                            abv = ab.rearrange("p (h c) -> p h c", h=4)
                            for h in range(4):
                                P.op("pe", lambda e, h=h, di=di, abv=abv: e.matmul(
                                    abv[:, h, :], lhsT=trt[:, 2 * di + 1, h, :], rhs=trt[:, 2 * di, h, :], start=True, stop=True),
                                    reads=[btr], writes=[bab], signal=(h == 3))
                            m = 0 if di == 0 else 2
                            P.op("dve", lambda e, di=di, abv=abv, m=m: e.tensor_tensor(
                                out=att[:, di], in0=abv, in1=tri[:, m, :].unsqueeze(1).broadcast_to([128, 4, 128]), op=ALU.mult),
                                reads=[bab, b_cst], writes=[bat])
                        P.op("pool", lambda e: e.tensor_tensor(out=att[:, 0], in0=att[:, 0], in1=att[:, 1], op=ALU.add),
                             reads=[bat], writes=[bat])
                    if isx:
                        opr = pring.next()
                        oap = pair_ap(opr).rearrange("p (h e) -> p h e", h=4)
                    sfbs = []
                    for ci in (0, 1):
                        cr = slice(ci * 64, (ci + 1) * 64)
                        if isx:
                            sfb, bsfb = sfb_r.next()
                            P.op("act", lambda e, sfb=sfb: e.activation(out=sfb, in_=Sf.rearrange("p h e -> p (h e)"), func=AF.Copy),
                                 reads=[b_Sf], writes=[bsfb])
                            sfbs.append((sfb, bsfb))
                        pr = pring.next()
                        pap = pair_ap(pr).rearrange("p (h e) -> p h e", h=4)
                        for h in range(4):
                            P.op("pe", lambda e, h=h, cr=cr, pap=pap: e.matmul(
                                pap[:, h, :], lhsT=pr5[cr, 2, h * 128:(h + 1) * 128], rhs=vv[cr, h * 256:(h + 1) * 256],
                                start=True, stop=True), reads=[bpr, bvv], writes=[pr[0][1], pr[1][1]], signal=(h == 3))
                        for h in range(4):
                            P.op("dve", lambda e, h=h, ci=ci, pap=pap: e.scalar_tensor_tensor(
                                out=Sf[:, h, :], in0=Sf[:, h, :], scalar=dec[:, h, ci:ci + 1], in1=pap[:, h, :],
                                op0=ALU.mult, op1=ALU.add), reads=[b_Sf, bdec, pr[0][1], pr[1][1]], writes=[b_Sf])
                    if not isx:
                        continue
                    for h in range(4):
                        P.op("pe", lambda e, h=h: e.matmul(oap[:, h, :], lhsT=att[:, 0, h, :], rhs=vv[:, h * 256:(h + 1) * 256],
                                                           start=True, stop=False),
                             reads=[bat, bvv], writes=[opr[0][1], opr[1][1]], signal=False)
                        for ci in (0, 1):
                            cs = slice(ci * 64, (ci + 1) * 64)
                            sfb, bsfb = sfbs[ci]
                            P.op("pe", lambda e, h=h, cs=cs, sfb=sfb: e.matmul(
                                oap[cs, h, :], lhsT=trt[:, 0, h, cs], rhs=sfb[:, h * 256:(h + 1) * 256], start=False, stop=False),
                                reads=[btr, bsfb], writes=[opr[0][1], opr[1][1]], signal=False)
                            P.op("pe", lambda e, h=h, cs=cs, ci=ci: e.matmul(
                                oap[cs, h, :], lhsT=trt[:, 2, h, cs], rhs=sbl[:, ci, h * 256:(h + 1) * 256], start=False,
                                stop=(ci == 1)), reads=[btr, bsbl], writes=[opr[0][1], opr[1][1]], signal=(ci == 1))
                    of, bof = of_r.next()
                    oq, boq = oq_r.next()
                    s2, bs2 = s2_r.next()
                    otok, bot = otok_r.next()
                    P.op("act", lambda e: e.activation(out=of, in_=oap, func=AF.Copy), reads=[opr[0][1], opr[1][1]], writes=[bof])
                    P.op("pool", lambda e: e.tensor_tensor(out=oq, in0=of, in1=of, op=ALU.mult), reads=[bof], writes=[boq])
                    P.op("dve", lambda e: e.tensor_reduce(out=s2[:, 0:4], in_=oq, axis=AXX, op=ALU.add), reads=[boq], writes=[bs2])
                    P.op("dve", lambda e: e.tensor_scalar(out=s2[:, 0:4], in0=s2[:, 0:4], scalar1=1.0 / 256, scalar2=EPS,
                                                          op0=ALU.mult, op1=ALU.add), reads=[bs2], writes=[bs2])
                    P.op("pool", lambda e: e.tensor_tensor(out=s2[:, 4:8], in0=s2[:, 0:4], in1=nhalf[:, 0:4], op=ALU.pow),
                         reads=[bs2, b_cb], writes=[bs2])
                    P.op("dve", lambda e: e.tensor_tensor(out=oq, in0=of, in1=s2[:, 4:8].unsqueeze(2).broadcast_to([128, 4, 256]),
                                                          op=ALU.mult), reads=[bof, bs2], writes=[boq])
                    P.op("pool", lambda e: e.tensor_tensor(out=of, in0=oq, in1=onbc, op=ALU.mult), reads=[boq, b_on], writes=[bof])
                    P.op("dve", lambda e: e.tensor_tensor(out=otok, in0=of.rearrange("p h e -> p (h e)"), in1=gg, op=ALU.mult),
                         reads=[bof, bgg], writes=[bot])
                    emit_outproj(b, t, otok, bot, 0, True, 1)

            for b in range(NB):
                phaseB1(b)
                phaseF1(b)
            P.barrier()

        for b in range(NB):
            for t in range(NT):
                if b_x1[b][t].w is not None:
                    P._dep("sp", b_x1[b][t].w)
        P.finish()
        print("program: nsem=%d" % P.nsem, {e: len(P.q[e]) for e in P.ENGS})
    return nc


def host_inputs(inputs, NB, cores):
    c = np.asarray(inputs["c"], np.float32)
    cctx = np.asarray(inputs["c_ctx"], np.float32)
    x = np.asarray(inputs["x"], np.float32)
    ctx = np.asarray(inputs["ctx"], np.float32)
    consts, rope = make_consts()
    pvec = np.zeros((PV_W,), np.float32)

    def put(off, a):
        a = np.asarray(a, np.float32).reshape(-1)
        pvec[off:off + a.size] = a

    put(PV_AQ, inputs["a_q_norm"]); put(PV_AK, inputs["a_k_norm"])
    put(PV_BQ, inputs["b_q_norm"]); put(PV_BK, inputs["b_k_norm"])
    put(PV_SINK, inputs["a_sink"]); put(PV_SUBLN, inputs["b_subln"])
    put(PV_LQ1, inputs["b_lambda_q1"]); put(PV_LK1, inputs["b_lambda_k1"])
    put(PV_LQ2, inputs["b_lambda_q2"]); put(PV_LK2, inputs["b_lambda_k2"])
    put(PV_ONORM, inputs["gla_out_norm"])
    ng = np.asarray(inputs["norm_g"], np.float32)
    normgT = np.ascontiguousarray(ng.reshape(2, 8, 128).transpose(0, 2, 1))
    shared = {
        "adaln_w": np.ascontiguousarray(inputs["adaln_w"], np.float32),
        "adaln_b": np.ascontiguousarray(inputs["adaln_b"], np.float32),
        "normgT": normgT,
        "w_out": np.ascontiguousarray(inputs["w_out"], np.float32),
        "ab_w_in": np.ascontiguousarray(np.asarray(inputs["ab_w_in"], np.float32)[0]),
        "gla_w_in": np.ascontiguousarray(np.asarray(inputs["gla_w_in"], np.float32)[0]),
        "gla_wa_f": np.ascontiguousarray(np.asarray(inputs["gla_wa_f"], np.float32)[0]),
        "gla_wa_b": np.ascontiguousarray(np.asarray(inputs["gla_wa_b"], np.float32)[0]),
        "pvec": pvec,
        "gla_ba_f": np.ascontiguousarray(np.asarray(inputs["gla_ba_f"], np.float32).reshape(1, 512)),
        "gla_ba_b": np.ascontiguousarray(np.asarray(inputs["gla_ba_b"], np.float32).reshape(1, 512)),
        "consts": consts,
        "rope": rope,
    }
    maps = []
    for ci in cores:
        rows = np.zeros((8, D), np.float32)
        rows[0:NB] = c[ci * NB:(ci + 1) * NB]
        rows[NB] = cctx
        cT = np.ascontiguousarray(rows.reshape(8, 8, 128).transpose(2, 1, 0))
        m = dict(shared)
        m["x"] = np.ascontiguousarray(np.concatenate([x[ci * NB:(ci + 1) * NB], ctx[ci * NB:(ci + 1) * NB]], axis=1))
        m["cT"] = cT
        maps.append(m)
    return maps


_NC_CACHE = {}


def kernel(**inputs):
    NB = 4
    if NB not in _NC_CACHE:
        _NC_CACHE[NB] = build(NB, stage=2)
    nc = _NC_CACHE[NB]
    maps = host_inputs(inputs, NB, list(range(8)))
    res = run_bass_kernel_spmd(nc, maps, core_ids=list(range(8)))
    out = np.concatenate([np.asarray(r["out"])[:, :T, :] for r in res.results], axis=0)
    return out.astype(np.float32)
```
